# Optimizing a Trainium2 kernel written in Bass

```python
import math
import jax, jax.numpy as jnp
from jax import lax
import numpy as np

D_MODEL = 1024
BATCH = 16
SEQ = 4096
DEPTH = 1

MIX_WIDTH = D_MODEL
GLA_WIDTH = MIX_WIDTH // 2
GLA_HEADS = 4
GLA_DV = GLA_WIDTH // GLA_HEADS
GLA_DK = GLA_DV // 2
GLA_KEY_WIDTH = GLA_HEADS * GLA_DK
GATE_RANK = 16
GATE_TAU = 16.0
CHUNK = 64
FNET_WIDTH = MIX_WIDTH - GLA_WIDTH
FNET_GROUPS = 4
FNET_GDIM = FNET_WIDTH // FNET_GROUPS
LN_EPS = 1e-5
RMS_EPS = 1e-6
DEEPNORM_ALPHA = (2.0 * DEPTH) ** 0.25
DEEPNORM_BETA = (8.0 * DEPTH) ** -0.25

_SPLITS = [GLA_KEY_WIDTH,
           GLA_KEY_WIDTH,
           GLA_WIDTH,
           GATE_RANK,
           GATE_RANK,
           GLA_WIDTH,
           FNET_WIDTH,
           FNET_WIDTH]
IN_WIDTH = sum(_SPLITS)
_OFFSETS = list(np.cumsum(_SPLITS)[:-1])

kernel_name = "hybrid_gla_fnet_deepnorm_encoder"


def _gla_chunked(q, k, v, log_a):
    B, S, H, DK = q.shape
    DV = v.shape[-1]
    n = S // CHUNK

    def to_chunks(t):
        return t.reshape(B, n, CHUNK, H, t.shape[-1]).transpose(1, 0, 3, 2, 4)

    qc, kc, vc, gc = (to_chunks(t) for t in (q, k, v, log_a))
    bc = jnp.cumsum(gc, axis=3)
    mask = jnp.tril(jnp.ones((CHUNK, CHUNK), dtype=bool))[:, :, None]

    def step(state, inp):
        qi, ki, vi, bi = inp
        b_last = bi[:, :, -1:, :]
        diff = bi[:, :, :, None, :] - bi[:, :, None, :, :]
        decay = jnp.exp(jnp.where(mask, diff, -jnp.inf))
        scores = jnp.einsum('bhid,bhjd,bhijd->bhij', qi, ki, decay)
        o = (jnp.einsum('bhij,bhjv->bhiv', scores, vi)
             + jnp.einsum('bhid,bhdv->bhiv', qi * jnp.exp(bi), state))
        k_dec = ki * jnp.exp(b_last - bi)
        state = (jnp.exp(b_last)[:, :, 0, :, None] * state
                 + jnp.einsum('bhjd,bhjv->bhdv', k_dec, vi))
        return state, o

    state0 = jnp.zeros((B, H, DK, DV), jnp.float32)
    _, o = lax.scan(step, state0, (qc, kc, vc, bc))
    return o.transpose(1, 0, 3, 2, 4).reshape(B, S, H, DV)


def _layernorm(x, g, b):
    xf = x.astype(jnp.float32)
    mu = jnp.mean(xf, axis=-1, keepdims=True)
    var = jnp.mean(jnp.square(xf - mu), axis=-1, keepdims=True)
    y = (xf - mu) * lax.rsqrt(var + LN_EPS) * g.astype(jnp.float32) + b.astype(jnp.float32)
    return y.astype(x.dtype)


def setup_inputs(seed: int = 0) -> dict:
    key = jax.random.key(seed)
    ks = jax.random.split(key, 12)
    x = jax.random.normal(ks[0], (BATCH, SEQ, D_MODEL), jnp.float32)
    w_in = jax.random.normal(ks[1], (DEPTH, D_MODEL, IN_WIDTH), jnp.float32) * D_MODEL ** -0.5
    v0, v1 = 2 * GLA_KEY_WIDTH, 2 * GLA_KEY_WIDTH + GLA_WIDTH
    w_in = w_in.at[:, :, v0:v1].multiply(DEEPNORM_BETA)
    w_gate_up_fwd = jax.random.normal(ks[2], (DEPTH, GATE_RANK, GLA_KEY_WIDTH), jnp.float32) * GATE_RANK ** -0.5
    b_gate_fwd = jax.random.normal(ks[3], (DEPTH, GLA_KEY_WIDTH), jnp.float32) * 0.1
    w_gate_up_bwd = jax.random.normal(ks[4], (DEPTH, GATE_RANK, GLA_KEY_WIDTH), jnp.float32) * GATE_RANK ** -0.5
    b_gate_bwd = jax.random.normal(ks[5], (DEPTH, GLA_KEY_WIDTH), jnp.float32) * 0.1
    gla_norm_g = 1.0 + 0.02 * jax.random.normal(ks[6], (DEPTH, GLA_WIDTH), jnp.float32)
    w_fnet = jax.random.normal(ks[7], (DEPTH, FNET_GROUPS, FNET_GDIM, FNET_GDIM), jnp.float32) * FNET_GDIM ** -0.5
    w_out = jax.random.normal(ks[8], (DEPTH, MIX_WIDTH, D_MODEL), jnp.float32) * (MIX_WIDTH ** -0.5 * DEEPNORM_BETA)
    ln_g = 1.0 + 0.02 * jax.random.normal(ks[9], (DEPTH, D_MODEL), jnp.float32)
    ln_b = 0.02 * jax.random.normal(ks[10], (DEPTH, D_MODEL), jnp.float32)
    return {"x": x, "w_in": w_in, "w_gate_up_fwd": w_gate_up_fwd, "b_gate_fwd": b_gate_fwd,
            "w_gate_up_bwd": w_gate_up_bwd, "b_gate_bwd": b_gate_bwd, "gla_norm_g": gla_norm_g,
            "w_fnet": w_fnet, "w_out": w_out, "ln_g": ln_g, "ln_b": ln_b}


def reference(x, w_in, w_gate_up_fwd, b_gate_fwd, w_gate_up_bwd, b_gate_bwd, gla_norm_g,
              w_fnet, w_out, ln_g, ln_b):
    B, S, _ = x.shape
    f32 = jnp.float32
    for l in range(DEPTH):
        proj = jnp.einsum('bsd,de->bse', x, w_in[l])
        q, k, v, gf_lr, gb_lr, z_gla, u, z_fnet = jnp.split(proj, _OFFSETS, axis=-1)

        qh = (q.astype(f32) * (GLA_DK ** -0.5)).reshape(B, S, GLA_HEADS, GLA_DK)
        kh = k.astype(f32).reshape(B, S, GLA_HEADS, GLA_DK)
        vh = v.astype(f32).reshape(B, S, GLA_HEADS, GLA_DV)
        log_af = jax.nn.log_sigmoid(
            jnp.einsum('bsr,rk->bsk', gf_lr.astype(f32), w_gate_up_fwd[l].astype(f32))
            + b_gate_fwd[l].astype(f32)) / GATE_TAU
        log_ab = jax.nn.log_sigmoid(
            jnp.einsum('bsr,rk->bsk', gb_lr.astype(f32), w_gate_up_bwd[l].astype(f32))
            + b_gate_bwd[l].astype(f32)) / GATE_TAU
        log_af = log_af.reshape(B, S, GLA_HEADS, GLA_DK)
        log_ab = log_ab.reshape(B, S, GLA_HEADS, GLA_DK)
        o_fwd = _gla_chunked(qh, kh, vh, log_af)
        flip = lambda t: jnp.flip(t, axis=1)
        o_bwd = flip(_gla_chunked(flip(qh), flip(kh), flip(vh), flip(log_ab)))
        o = o_fwd + o_bwd
        o = o * lax.rsqrt(jnp.mean(jnp.square(o), axis=-1, keepdims=True) + RMS_EPS)
        o = o.reshape(B, S, GLA_WIDTH) * gla_norm_g[l].astype(f32)
        y_gla = o * jax.nn.silu(z_gla.astype(f32))

        ug = u.astype(f32).reshape(B, S, FNET_GROUPS, FNET_GDIM)
        uf = jnp.fft.fft2(ug, axes=(1, 3), norm='ortho').real.astype(f32)
        uf = jnp.einsum('bsgc,gcd->bsgd', uf, w_fnet[l].astype(f32)).reshape(B, S, FNET_WIDTH)
        y_fnet = uf * jax.nn.silu(z_fnet.astype(f32))

        mix = jnp.concatenate([y_gla, y_fnet], axis=-1).astype(x.dtype)
        y = jnp.einsum('bse,ed->bsd', mix, w_out[l])

        x = _layernorm(DEEPNORM_ALPHA * x + y, ln_g[l], ln_b[l])
    return x
```

```python
import math
from contextlib import ExitStack

import numpy as np
import ml_dtypes

import concourse.bass as bass
import concourse.mybir as mybir
from concourse.bass_utils import run_bass_kernel_spmd

F32 = mybir.dt.float32
BF16 = mybir.dt.bfloat16
AF = mybir.ActivationFunctionType
ALU = mybir.AluOpType

D = 1024
NCORES = 8
E_IN = 2592
OQ, OK_, OV, OGF, OGB, OZG, OU, OZF = 0, 256, 512, 1024, 1040, 1056, 1568, 2080
ALPHA = 2.0 ** 0.25
LN_EPS = 1e-5
RMS_EPS = 1e-6
DVE_RUNS = True
PRIO1 = (0.0, 0.0, 0.0)
PRIO2 = (0.0, 0.0, 0.0)
SCHED2 = False
LIST_SCHED = True
SAME_SYNC = True


class Res:
    __slots__ = ("w", "r", "name", "excl")

    def __init__(self, name="", excl=False):
        self.w = None
        self.r = {}
        self.name = name
        self.excl = excl


class Sched:
    ENGS = ("pe", "act", "dve", "pool", "sp")

    def __init__(self):
        self.ops = {e: [] for e in self.ENGS}
        self.cnt = {}
        self.waited = {e: {} for e in self.ENGS}
        self.sems = {}
        self.rec = None
        self.vfree = {}
        self.vw = {}
        self.vr = {}

    def record(self, f, *a):
        assert self.rec is None
        self.rec = []
        f(*a)
        r, self.rec = self.rec, None
        return r

    def zipper(self, *lists, prio=None, sched=True):
        idx = [0] * len(lists)
        while True:
            best, bi = None, -1
            for i, l in enumerate(lists):
                if idx[i] < len(l):
                    eng, fn, reads, writes, dma, cost = l[idx[i]]
                    if LIST_SCHED and sched:
                        key = (self._est(eng, reads, writes, dma) - (prio[i] if prio else 0.0),
                               (idx[i] + 0.5) / len(l))
                    else:
                        key = ((idx[i] + 0.5) / len(l) + (prio[i] if prio else 0.0),)
                    if best is None or key < best:
                        best, bi = key, i
            if bi < 0:
                break
            self.op(*lists[bi][idx[bi]])
            idx[bi] += 1

    DEF_COST = {"pe": 0.5, "act": 0.6, "dve": 0.45, "pool": 1.3, "sp": 0.1}
    HOP = 0.35

    def _est(self, eng, reads, writes, dma):
        t = self.vfree.get(eng, 0.0)
        key = dma if dma is not None else eng
        for r in reads:
            w = self.vw.get(id(r))
            if w is not None:
                t = max(t, w[1] + (self.HOP if w[0] != eng else 0.0))
        for w_ in writes:
            w = self.vw.get(id(w_))
            if w is not None:
                t = max(t, w[1] + (self.HOP if w[0] != eng else 0.0))
            for k, v in self.vr.get(id(w_), {}).items():
                t = max(t, v + (self.HOP if k != eng else 0.0))
        return t

    def _commit(self, eng, reads, writes, dma, cost):
        t0 = self._est(eng, reads, writes, dma)
        if dma is not None:
            self.vfree[eng] = t0 + 0.1
            t1 = t0 + (cost if cost is not None else 3.0)
            who = "dma"
        else:
            t1 = t0 + (cost if cost is not None else self.DEF_COST[eng])
            self.vfree[eng] = t1
            who = eng
        for r in reads:
            d = self.vr.setdefault(id(r), {})
            d[who] = max(d.get(who, 0.0), t1)
        for w_ in writes:
            self.vw[id(w_)] = (who, t1)
            self.vr[id(w_)] = {}

    def op(self, eng, fn, reads=(), writes=(), dma=None, cost=None):
        if self.rec is not None:
            self.rec.append((eng, fn, tuple(reads), tuple(writes), dma, cost))
            return None
        self._commit(eng, reads, writes, dma, cost)
        deps = {}

        def add(tok):
            if tok is None:
                return
            k, v = tok
            if deps.get(k, 0) < v:
                deps[k] = v

        own = "e_" + eng
        for r in reads:
            add(r.w)
            if r.excl:
                for k, v in r.r.items():
                    if k != own:
                        add((k, v))
        for w in writes:
            add(w.w)
            for k, v in w.r.items():
                add((k, v))
        own = "e_" + eng
        waits = []
        for k, v in deps.items():
            if k == own and (eng == "pe" or not SAME_SYNC):
                continue
            if self.waited[eng].get(k, 0) < v:
                self.waited[eng][k] = v
                waits.append((k, v))
        if dma is None:
            key, inc = own, 1
        else:
            key, inc = dma, 16
        val = self.cnt.get(key, 0) + inc
        self.cnt[key] = val
        tok = (key, val)
        self.ops[eng].append((waits, fn, key, inc))
        for r in reads:
            if r.r.get(key, 0) < val:
                r.r[key] = val
        for w in writes:
            w.w = tok
            w.r = {}
        return tok

    def wait_all(self, eng, toks):
        waits = []
        for k, v in toks:
            if self.waited[eng].get(k, 0) < v:
                self.waited[eng][k] = v
                waits.append((k, v))
        if waits:
            self.ops[eng].append((waits, None, None, 0))

    def emit(self, eng, e):
        for waits, fn, key, inc in self.ops[eng]:
            for k, v in waits:
                e.wait_ge(self.sems[k], v)
            if fn is not None:
                ins = fn(e)
                ins.then_inc(self.sems[key], inc)


def _tables(S):
    N1 = 128
    N2 = S // N1
    Q = 128 // N2
    bf = ml_dtypes.bfloat16
    scale = 1.0 / math.sqrt(S * 128.0)
    c = np.arange(128)
    ang = -2.0 * np.pi * np.outer(c, c) / 128.0
    FC = np.concatenate([np.cos(ang) * scale, np.sin(ang) * scale], axis=1)
    n1 = np.arange(N1)
    k1 = np.arange(N1)
    T = np.zeros((N2, 128, 2, 256), np.float64)
    for n2 in range(N2):
        a = -2.0 * np.pi * np.outer(N2 * n1 + n2, k1) / S
        tre, tim = np.cos(a), np.sin(a)
        perm = np.array([N2 * q + kr for kr in range(N2) for q in range(Q)])
        tre, tim = tre[:, perm], tim[:, perm]
        T[n2, :, 0, :128] = tre
        T[n2, :, 0, 128:] = tim
        T[n2, :, 1, :128] = -tim
        T[n2, :, 1, 128:] = tre
    BD = np.zeros((128, 2, 128), np.float64)
    for q in range(Q):
        for n2 in range(N2):
            for k2 in range(N2):
                a = -2.0 * np.pi * n2 * k2 / N2
                BD[n2 * Q + q, 0, q + Q * k2] = np.cos(a)
                BD[n2 * Q + q, 1, q + Q * k2] = -np.sin(a)
    j = np.arange(128)[:, None]
    i = np.arange(128)[None, :]
    tri = np.concatenate([(j <= i), (j >= i)], axis=1).astype(np.float32)
    ident = np.eye(128, dtype=np.float32).astype(bf)
    return dict(
        c_fc=FC.astype(np.float32).astype(bf),
        c_tt=T.reshape(N2, 128, 512).astype(np.float32).astype(bf),
        c_bd=BD.reshape(128, 256).astype(np.float32).astype(bf),
        c_tri=tri,
        c_id=ident,
    )


def build(S=4096, NSEQ=2):
    NT = S // 128
    N2 = S // 128
    Q = 128 // N2
    KRB = min(4, N2)
    nc = bass.Bass("TRN2", target_bir_lowering=False)
    TOK = NSEQ * S

    def din(name, shape, dt=F32):
        return nc.dram_tensor(name, shape, dt, kind="ExternalInput").ap()

    x_d = din("x", [TOK, D])
    win_d = din("w_in", [D, E_IN])
    wout_d = din("w_out", [D, D])
    wup_d = din("wup", [33, 512])
    gain_d = din("gain", [1, 512])
    wf_d = din("w_fnet", [128, 512])
    lng_d = din("ln_g", [1, D])
    lnb_d = din("ln_b", [1, D])
    fc_d = din("c_fc", [128, 256], BF16)
    tt_d = din("c_tt", [N2, 128, 512], BF16)
    bd_d = din("c_bd", [128, 256], BF16)
    tri_d = din("c_tri", [128, 256])
    id_d = din("c_id", [128, 128], BF16)
    out_d = nc.dram_tensor("out", [TOK, D], F32, kind="ExternalOutput").ap()

    sc = Sched()
    es = ExitStack()
    with es:
        def sb(name, shape, dt):
            return es.enter_context(nc.sbuf_tensor("s_" + name, shape, dt))[:]

        def ps(name, shape, dt):
            return es.enter_context(nc.psum_tensor("p_" + name, shape, dt))[:]

        win_bf = sb("win_bf", [128, 8, E_IN], BF16)
        wout_bf = sb("wout_bf", [128, 8, D], BF16)
        wup_bf = sb("wup_bf", [33, 512], BF16)
        tri = sb("tri", [128, 256], F32)
        ident = sb("ident", [128, 128], BF16)
        ones = sb("ones", [128, 1], F32)
        gain = sb("gain", [128, 512], F32)
        lng = sb("lng", [128, D], F32)
        lnb = sb("lnb", [128, D], F32)
        wf_bf = sb("wf_bf", [128, 512], BF16)
        fc = sb("fc", [128, 256], BF16)
        bd = sb("bd", [128, 256], BF16)
        UF = sb("UF", [128, 4, S], BF16)
        Sbs = sb("Sbs", [128, NT, 256], BF16)
        Zg = sb("Zg", [128, 2 * N2 * 128], BF16)
        Sf = sb("Sf", [128, 256], F32)
        Sf_bf = sb("Sf_bf", [128, 256], BF16)
        Sbst = sb("Sbst", [128, 256], F32)
        stmp = sb("stmp", [128, 256], F32)
        NXS = 4
        big = sb("big", [128, NXS + 3, D], F32)
        xb = sb("xb", [128, D], BF16)
        xT2 = sb("xT", [128, 3, 8, 128], BF16)
        gT2 = sb("gT", [33, 2, 128], BF16)
        ksb2 = sb("ksb", [128, 2, 256], F32)
        e1 = sb("e1", [128, 512], F32)
        sdec2 = sb("sdec", [128, 2, 4], F32)
        QK2 = sb("QK", [128, 2, 4, 256], BF16)
        vbf2 = sb("v_bf", [128, 3, 512], BF16)
        KDT2 = sb("KDT", [128, 2, 4, 128], BF16)
        QEz2 = sb("QEz", [128, 2, 2, 4, 128], BF16)
        ss = sb("ss", [128, 4], F32)
        rstd = sb("rstd", [128, 4], F32)
        if 2 * N2 * 128 >= 6656:
            ef2 = Zg[:, 0:2048].bitcast(F32).rearrange("p (a b) -> p a b", a=2)
            eg = Zg[:, 2048:3072].bitcast(F32)
            scm = Zg[:, 3072:4096].rearrange("p (a b) -> p a b", a=2)
            mixT2 = Zg[:, 4096:6144].rearrange("p (a b c) -> p a b c", a=2, b=8)
            yg = Zg[:, 6144:6656]
        else:
            ef2 = sb("ef", [128, 2, 512], F32)
            eg = sb("eg", [128, 512], F32)
            scm = sb("scm", [128, 2, 512], BF16)
            mixT2 = sb("mixT", [128, 2, 8, 128], BF16)
            yg = sb("yg", [128, 512], BF16)
        bst = sb("bst", [128, 2, 6], F32)
        mv = sb("mv", [128, 2], F32)
        lnr = sb("lnr", [128, 2], F32)
        junk = sb("junk", [128, 128], F32)
        EqEk = sb("EqEk", [128, 2, 512], F32)
        Eq = EqEk[:, 0, :]
        Ek = EqEk[:, 1, :]
        tts = EqEk.rearrange("p a b -> p (a b)").bitcast(BF16).rearrange("p (s n) -> p s n", s=4)
        Vs = e1.bitcast(BF16).rearrange("p (s n) -> p s n", s=4)
        Ys = QK2.rearrange("p a b c -> p (a b c)").rearrange("p (s n) -> p s n", s=4)

        NROT = 5
        rot = [ps("rot%d" % i, [128, 512], F32) for i in range(NROT)]
        misc = ps("misc", [128, 512], F32)
        pb0 = ps("pb0", [128, 8, 128], BF16)
        pb1 = ps("pb1", [128, 8, 128], BF16)
        ygT_ps = misc[:, 256:512].bitcast(BF16).rearrange("p (h i) -> p h i", h=4)

        R = {}

        def res(name):
            if name.startswith("m_"):
                name = "misc"
            if name not in R:
                R[name] = Res(name, excl=(name == "misc" or name.startswith("rot") or name.startswith("pb")))
            return R[name]

        rot_i = [0]

        def nextrot():
            i = rot_i[0] % NROT
            rot_i[0] += 1
            return rot[i], res("rot%d" % i)

        def bank(i):
            return rot[i], res("rot%d" % i)

        def full_barrier():
            toks = [(k, v) for k, v in sc.cnt.items()]
            for eng in Sched.ENGS:
                sc.wait_all(eng, toks)

        def dma(eng, out, in_, reads, writes, key):
            return sc.op(eng, lambda e: e.dma_start(out=out, in_=in_), reads, writes, dma=key)

        bigflat = big.rearrange("p a b -> p (a b)")
        HW = E_IN // 2
        stage = [bigflat[:, i * HW:(i + 1) * HW] for i in range(4)]
        r_stage = [res("stage%d" % i) for i in range(4)]
        si = 0
        cast_eng = ["act", "dve", "act", "dve"]

        def cast(eng, o_, i_, reads, writes):
            if eng == "act":
                sc.op("act", lambda e: e.activation(out=o_, in_=i_, func=AF.Copy), reads, writes)
            else:
                sc.op(eng, lambda e: e.tensor_copy(out=o_, in_=i_), reads, writes)

        for kc in range(8):
            for hf in range(2):
                s_ = si % 4
                dma("sp", stage[s_], win_d[kc * 128:(kc + 1) * 128, hf * HW:(hf + 1) * HW],
                    [], [r_stage[s_]], "d_stage%d" % s_)
                cast(cast_eng[s_], win_bf[:, kc, hf * HW:(hf + 1) * HW], stage[s_],
                     [r_stage[s_]], [res("win_bf_" + cast_eng[s_])])
                si += 1
        for kc in range(8):
            s_ = si % 4
            dma("sp", stage[s_][:, 0:D], wout_d[kc * 128:(kc + 1) * 128, :],
                [], [r_stage[s_]], "d_stage%d" % s_)
            cast(cast_eng[s_], wout_bf[:, kc, :], stage[s_][:, 0:D],
                 [r_stage[s_]], [res("wout_bf_" + cast_eng[s_])])
            si += 1
        cst = res("consts")
        dma("pool", e1[0:33, :], wup_d, [], [res("wup_f")], "d_c0")
        dma("pool", Eq, wf_d, [], [res("wf_f")], "d_c1")
        dma("pool", tri, tri_d, [], [cst], "d_c2")
        dma("pool", ident, id_d, [], [cst], "d_c2")
        dma("pool", fc, fc_d, [], [cst], "d_c2")
        dma("pool", bd, bd_d, [], [cst], "d_c2")
        dma("pool", gain, gain_d.partition_broadcast(128), [], [cst], "d_c2")
        dma("pool", lng, lng_d.partition_broadcast(128), [], [cst], "d_c2")
        dma("pool", lnb, lnb_d.partition_broadcast(128), [], [cst], "d_c2")
        sc.op("act", lambda e: e.activation(out=wup_bf, in_=e1[0:33, :], func=AF.Copy),
              [res("wup_f")], [res("wup_bf")])
        sc.op("act", lambda e: e.activation(out=wf_bf, in_=Eq, func=AF.Copy),
              [res("wf_f")], [res("wf_bf")])
        sc.op("dve", lambda e: e.memset(ones, 1.0), [], [cst])
        sc.op("dve", lambda e: e.memset(gT2[32:33, :, :], 1.0), [], [res("gT_ones")])
        sc.op("dve", lambda e: e.memset(QEz2, 0.0), [], [res("QEz_init")])
        init_toks = [(k, v) for k, v in sc.cnt.items()]
        for eng in Sched.ENGS:
            sc.wait_all(eng, init_toks)

        r_x = [res("x%d" % i) for i in range(NXS)]
        xs = [big[:, i, :] for i in range(NXS)]
        ot = [big[:, NXS + i, :] for i in range(2)]
        r_ot = [res("ot0"), res("ot1")]
        rt = big[:, NXS + 2, :]
        xcount = [0]
        ocount = [0]

        def load_x(row0):
            s_ = xcount[0] % NXS
            xcount[0] += 1
            dma("sp", xs[s_], x_d[row0:row0 + 128, :], [], [r_x[s_]], "d_x%d" % s_)
            return s_

        def make_xT(s_, xp):
            par = xp
            sc.op("pool", lambda e: e.tensor_copy(out=xb[:, 0:512], in_=xs[s_][:, 0:512]),
                  [r_x[s_]], [res("xb0")], cost=1.85)
            sc.op("dve",
                  lambda e: e.tensor_copy(out=xb[:, 512:1024], in_=xs[s_][:, 512:1024]),
                  [r_x[s_]], [res("xb1")])

            def f(e):
                for kc in range(8):
                    ins = e.transpose(out=pb0[:, kc, :], in_=xb[:, kc * 128:(kc + 1) * 128],
                                      identity=ident)
                return ins
            sc.op("pe", f, [res("xb0"), res("xb1")], [res("pb0")], cost=0.6)
            sc.op("dve", lambda e: e.tensor_copy(out=xT2[:, par], in_=pb0), [res("pb0")], [res("xT%d" % par)], cost=0.7)

        def proj_tok(par, c0, n, dst, rdst):
            COSTP = 8 * n / 2400.0 + 0.05
            def f(e):
                for kc in range(8):
                    ins = e.matmul(dst, lhsT=xT2[:, par, kc, :], rhs=win_bf[:, kc, c0:c0 + n],
                                   start=(kc == 0), stop=(kc == 7))
                return ins
            sc.op("pe", f, [res("xT%d" % par)], [rdst], cost=COSTP)

        def proj_feat(par, c0, m, dst, rdst):
            COSTP = 8 * 0.06 + 0.05
            def f(e):
                for kc in range(8):
                    ins = e.matmul(dst, lhsT=win_bf[:, kc, c0:c0 + m], rhs=xT2[:, par, kc, :],
                                   start=(kc == 0), stop=(kc == 7))
                return ins
            sc.op("pe", f, [res("xT%d" % par)], [rdst], cost=COSTP)

        def gates(par, ncols, col0, bk):
            zb, rzb = bank(bk)
            sc.op("pe", lambda e: e.matmul(zb[:, 0:ncols], lhsT=gT2[0:33, par, :],
                                           rhs=wup_bf[0:33, col0:col0 + ncols], start=True, stop=True),
                  [res("gT%d" % par), res("gT_ones")], [rzb])
            sc.op("act", lambda e: e.activation(out=e1[:, 0:ncols], in_=zb[:, 0:ncols], func=AF.Exp,
                                                scale=-1.0), [rzb], [res("e1")])
            sc.op("act", lambda e: e.activation(out=e1[:, 0:ncols], in_=e1[:, 0:ncols], func=AF.Ln,
                                                bias=1.0), [res("e1")], [res("e1")])

        def state_update(kd_ap, rkd, vb_ap, rvb, S_ap, rS, sd_ap, rsd, bk):
            ub, rub = bank(bk)
            u4 = ub.rearrange("p (c h v) -> p c h v", c=2, h=2)

            def f2(e):
                ins = None
                for c in range(2):
                    for hh in range(2):
                        h = 2 * c + hh
                        ins = e.matmul(u4[:, c, hh, :], lhsT=kd_ap[:, c * 128:(c + 1) * 128],
                                       rhs=vb_ap[:, h * 128:(h + 1) * 128], start=True, stop=True)
                return ins
            sc.op("pe", f2, [rkd, rvb], [rub])
            Sv = S_ap.rearrange("p (c v) -> p c v", c=2)
            tv = stmp.rearrange("p (c v) -> p c v", c=2)
            for hh in range(2):
                p0 = hh * 64
                sc.op("dve", lambda e, p0=p0, hh=hh: e.tensor_tensor(
                    out=tv[p0:p0 + 64], in0=Sv[p0:p0 + 64], in1=u4[p0:p0 + 64, :, hh, :], op=ALU.add),
                    [rS, rub], [res("stmp%d" % hh)])
            for c in range(2):
                sc.op("dve", lambda e, c=c: e.tensor_scalar(
                    out=Sv[:, c, :], in0=tv[:, c, :], scalar1=sd_ap[:, c:c + 1],
                    scalar2=None, op0=ALU.mult), [res("stmp0"), res("stmp1"), rsd], [rS])

        UFv = UF.rearrange("p g (b a) -> p g b a", b=N2)
        FNv = UF
        ZN = 2 * N2 * 128
        Zg1 = bigflat.bitcast(BF16)[:, 0:ZN]
        Zv2 = [z_.rearrange("p (r k n q) -> p r k n q", r=2, k=N2, q=Q) for z_ in (Zg, Zg1)]
        fbank = [[bank(0), bank(1), bank(2), bank(3)],
                 [bank(4), (misc, res("misc")),
                  (pb0.rearrange("p a b -> p (a b)").bitcast(F32), res("pb0")),
                  (pb1.rearrange("p a b -> p (a b)").bitcast(F32), res("pb1"))]]
        FNp = UF.rearrange("p g (k m) -> p g k m", k=N2)
        r_uf = [res("uf%d" % g) for g in range(4)]

        for s in range(NSEQ):
            base = s * S
            sc.op("dve", lambda e: e.memset(Sbst, 0.0), [], [res("Sbst")])
            sc.op("dve", lambda e: e.memset(Sf, 0.0), [], [res("Sf")])
            sc.op("dve", lambda e: e.memset(Sf_bf, 0.0), [], [res("Sf_bf")])
            order = list(range(NT - 1, -1, -1))
            xslot = {}

            def G1(oi):
                t = order[oi]
                par = oi % 2
                xp = oi % 3
                if oi == 0:
                    xslot[0] = load_x(base + order[0] * 128)
                if oi + 1 < NT:
                    xslot[oi + 1] = load_x(base + order[oi + 1] * 128)
                make_xT(xslot[oi], xp)
                kb, rkb = bank(0)
                proj_feat(xp, OGF, 32, misc[0:32, 0:128], res("m_g"))
                proj_tok(xp, OK_, 256, kb[:, 0:256], rkb)
                vb, rvb = bank(1)
                proj_tok(xp, OV, 512, vb, rvb)
                ub, rub = bank(2)
                for g in range(4):
                    proj_feat(xp, OU + g * 128, 128, ub[:, g * 128:(g + 1) * 128], rub)
                sc.op("act", lambda e: e.activation(out=gT2[0:32, par, :], in_=misc[0:32, 0:128], func=AF.Copy),
                      [res("m_g")], [res("gT%d" % par)])
                sc.op("act", lambda e: e.activation(out=ksb2[:, par, :], in_=kb[:, 0:256], func=AF.Copy),
                      [rkb], [res("ksb%d" % par)])
                sc.op("act", lambda e: e.activation(out=vbf2[:, xp, :], in_=vb, func=AF.Copy),
                      [rvb], [res("v_bf%d" % xp)])
                src_ = ub.rearrange("p (g a b) -> p g a b", g=4, b=N2)
                dst_ = UFv[:, :, :, t * Q:(t + 1) * Q].rearrange("p g b a -> p g a b")
                sc.op("act", lambda e: e.activation(out=dst_, in_=src_, func=AF.Copy), [rub], r_uf, cost=1.5)

            def G2a(oi):
                t = order[oi]
                par = oi % 2
                gates(par, 256, 256, 3)
                cb, rcb = bank(3)
                sc.op("pe", lambda e: e.matmul(cb[:, 0:256], lhsT=tri[:, 128:256], rhs=e1[:, 0:256],
                                               start=True, stop=True), [res("e1")], [rcb])

                def ftot(e):
                    for c in range(2):
                        ins = e.matmul(misc[:, 128 + c:129 + c], lhsT=e1[:, c * 128:(c + 1) * 128],
                                       rhs=ones[:, 0:1], start=True, stop=True)
                    return ins
                sc.op("pe", ftot, [res("e1")], [res("m_tot")])
                sc.op("act", lambda e: e.activation(out=Ek[:, 0:256], in_=cb[:, 0:256], func=AF.Exp,
                                                    scale=1.0 / 16.0), [rcb], [res("Ek")])
                sc.op("act", lambda e: e.activation(out=sdec2[:, par, 0:2], in_=misc[:, 128:130], func=AF.Exp,
                                                    scale=-1.0 / 16.0), [res("m_tot")], [res("sdec%d" % par)])
                sc.op("dve", lambda e: e.tensor_tensor(out=QK2[:, par, 3, :], in0=ksb2[:, par, :],
                                                       in1=Ek[:, 0:256], op=ALU.mult),
                      [res("ksb%d" % par), res("Ek")], [res("QK3_%d" % par)])

            def G2b(oi):
                t = order[oi]
                par = oi % 2
                xp = oi % 3
                sc.op("pool", lambda e: e.tensor_copy(out=Sbs[:, t, :], in_=Sbst),
                      [res("Sbst")], [res("Sbs")])
                state_update(QK2[:, par, 3, :], res("QK3_%d" % par), vbf2[:, xp, :], res("v_bf%d" % xp),
                             Sbst, res("Sbst"), sdec2[:, par, 0:2], res("sdec%d" % par), 4)

            G1(0)
            if NT > 1:
                sc.zipper(sc.record(G1, 1), sc.record(G2a, 0))
            else:
                G2a(0)
            for oi in range(NT):
                lists = []
                pr = []
                if oi + 2 < NT:
                    lists.append(sc.record(G1, oi + 2))
                    pr.append(PRIO1[0])
                if oi + 1 < NT:
                    lists.append(sc.record(G2a, oi + 1))
                    pr.append(PRIO1[1])
                lists.append(sc.record(G2b, oi))
                pr.append(PRIO1[2])
                sc.zipper(*lists, prio=pr, sched=True)

            full_barrier()

            def fft_chain(g, c):
                Zc = Zv2[c]
                bA, bB, bT, bS = fbank[c]
                ttl = [0]

                def stepA(n2):
                    tsl = 2 * c + ttl[0] % 2
                    ttl[0] += 1
                    dma("sp", tts[:, tsl, :], tt_d[n2], [], [res("tts%d" % tsl)], "d_tt%d" % tsl)
                    ab, rab = bA
                    sc.op("pe", lambda e: e.matmul(ab[:, 0:256], lhsT=UFv[:, g, n2, :], rhs=fc,
                                                   start=True, stop=True), [r_uf[g]], [rab])
                    vsl = 2 * c + n2 % 2
                    rv = res("Vs%d" % vsl)
                    sc.op("dve", lambda e: e.tensor_copy(out=Vs[:, vsl, :], in_=ab[:, 0:256]), [rab], [rv])
                    return tsl, vsl

                def stepB(n2, tsl, vsl):
                    rv = res("Vs%d" % vsl)
                    bb, rbb = bB

                    def fB(e):
                        e.matmul(bb[:, 0:256], lhsT=Vs[:, vsl, 0:128], rhs=tts[:, tsl, 0:256],
                                 start=True, stop=False)
                        return e.matmul(bb[:, 0:256], lhsT=Vs[:, vsl, 128:256], rhs=tts[:, tsl, 256:512],
                                        start=False, stop=True)
                    sc.op("pe", fB, [rv, res("tts%d" % tsl)], [rbb])
                    dstz = Zc[:, :, :, n2, :]
                    srcz = bb[:, 0:256].rearrange("p (r k q) -> p r k q", r=2, q=Q)
                    if n2 % 2 == 0 or not DVE_RUNS:
                        sc.op("act", lambda e: e.activation(out=dstz, in_=srcz, func=AF.Copy),
                              [rbb], [res("Zg_a%d" % c)])
                    else:
                        sc.op("dve", lambda e: e.tensor_copy(out=dstz, in_=srcz),
                              [rbb], [res("Zg_d%d" % c)])

                pendA = stepA(0)
                for n2 in range(N2):
                    curA = pendA
                    if n2 + 1 < N2:
                        pendA = stepA(n2 + 1)
                    stepB(n2, *curA)

                NP = N2 // 2

                def stepT(i):
                    tb, rtb = bT
                    ysl = 2 * c + i % 2
                    ry = res("Ys%d" % ysl)

                    def fT(e):
                        ins = None
                        for j in range(2):
                            kr = 2 * i + j
                            for r_ in range(2):
                                ins = e.matmul(tb[:, (j * 2 + r_) * 128:(j * 2 + r_ + 1) * 128],
                                               lhsT=Zc[:, r_, kr, :, :].rearrange("p n q -> p (n q)"),
                                               rhs=wf_bf[:, g * 128:(g + 1) * 128], start=True, stop=True)
                        return ins
                    sc.op("pe", fT, [res("Zg_a%d" % c), res("Zg_d%d" % c), res("wf_bf")], [rtb])
                    if i % 2 == 0:
                        sc.op("dve", lambda e: e.tensor_copy(out=Ys[:, ysl, :], in_=tb), [rtb], [ry])
                    else:
                        sc.op("act", lambda e: e.activation(out=Ys[:, ysl, :], in_=tb, func=AF.Copy), [rtb], [ry])
                    return ysl

                def stepS(i, ysl):
                    ry = res("Ys%d" % ysl)
                    sb_, rsb = bS
                    kk = (i % 2) * 2

                    def fS(e):
                        ins = None
                        for j in range(2):
                            o_ = sb_[:, (kk + j) * 128:(kk + j + 1) * 128]
                            e.matmul(o_, lhsT=Ys[:, ysl, (j * 2) * 128:(j * 2 + 1) * 128], rhs=bd[:, 0:128],
                                     start=True, stop=False)
                            ins = e.matmul(o_, lhsT=Ys[:, ysl, (j * 2 + 1) * 128:(j * 2 + 2) * 128],
                                           rhs=bd[:, 128:256], start=False, stop=True)
                        return ins
                    sc.op("pe", fS, [ry], [rsb])
                    if i % 2 == 1 or i == NP - 1:
                        k0 = (i // 2) * 4
                        nk = kk + 2
                        dsts = FNp[:, g, k0:k0 + nk, :].rearrange("p k m -> p (k m)")
                        srcs = sb_[:, 0:nk * 128]
                        if (i // 2) % 2 == 0:
                            sc.op("act", lambda e: e.activation(out=dsts, in_=srcs, func=AF.Copy), [rsb], [r_uf[g]])
                        else:
                            sc.op("dve", lambda e: e.tensor_copy(out=dsts, in_=srcs), [rsb], [r_uf[g]])

                pendT = stepT(0)
                for i in range(NP):
                    curT = pendT
                    if i + 1 < NP:
                        pendT = stepT(i + 1)
                    stepS(i, curT)

            for g0 in (0, 2):
                sc.zipper(sc.record(fft_chain, g0, 0), sc.record(fft_chain, g0 + 1, 1))

            full_barrier()
            xslot = {}

            def H1(t):
                par = t % 2
                xp = t % 3
                if t == 0:
                    xslot[0] = load_x(base)
                if t + 1 < NT:
                    xslot[t + 1] = load_x(base + (t + 1) * 128)
                make_xT(xslot[t], xp)
                qkb, rqk = bank(0)
                proj_feat(xp, OGF, 32, misc[0:32, 0:128], res("m_g"))
                proj_tok(xp, OQ, 512, qkb, rqk)
                sc.op("act", lambda e: e.activation(out=gT2[0:32, par, :], in_=misc[0:32, 0:128], func=AF.Copy),
                      [res("m_g")], [res("gT%d" % par)])
                gates(par, 512, 0, 1)
                cb, rcb = bank(1)

                def fcum(e):
                    e.matmul(cb[:, 0:256], lhsT=tri[:, 0:128], rhs=e1[:, 0:256], start=True, stop=True)
                    return e.matmul(cb[:, 256:512], lhsT=tri[:, 128:256], rhs=e1[:, 256:512],
                                    start=True, stop=True)
                sc.op("pe", fcum, [res("e1")], [rcb])

                def ftot2(e):
                    for c in range(4):
                        ins = e.matmul(misc[:, 128 + c:129 + c], lhsT=e1[:, c * 128:(c + 1) * 128],
                                       rhs=ones[:, 0:1], start=True, stop=True)
                    return ins
                sc.op("pe", ftot2, [res("e1")], [res("m_tot")])
                sc.op("act", lambda e: e.activation(out=Eq, in_=cb, func=AF.Exp, scale=-1.0 / 16.0,
                                                    bias=math.log(0.125)), [rcb], [res("Eq")])
                sc.op("act", lambda e: e.activation(out=Ek, in_=cb, func=AF.Exp, scale=1.0 / 16.0),
                      [rcb], [res("Ek")])
                sc.op("act", lambda e: e.activation(out=sdec2[:, par, 0:4], in_=misc[:, 128:132], func=AF.Exp,
                                                    scale=-1.0 / 16.0), [res("m_tot")], [res("sdec%d" % par)])
                for w2 in range(2):
                    E_ = Eq if w2 == 0 else Ek
                    sc.op("dve", lambda e, w2=w2, E_=E_: e.tensor_tensor(
                        out=QK2[:, par, 2 * w2:2 * w2 + 2, :],
                        in0=qkb[:, w2 * 256:(w2 + 1) * 256].unsqueeze(1).to_broadcast([128, 2, 256]),
                        in1=E_.rearrange("p (a b) -> p a b", a=2), op=ALU.mult),
                        [rqk, res("Eq"), res("Ek")],
                        [res("QK%d_%d" % (2 * w2, par)), res("QK%d_%d" % (2 * w2 + 1, par))], cost=0.7)

                vb, rvb = bank(1)
                proj_tok(xp, OV, 512, vb, rvb)
                sc.op("act", lambda e: e.activation(out=vbf2[:, par, :], in_=vb, func=AF.Copy),
                      [rvb], [res("v_bf%d" % par)])

                def ftr(e):
                    ins = None
                    for w_ in range(4):
                        for c in range(2):
                            ins = e.transpose(out=pb1[:, w_ * 2 + c, :],
                                              in_=QK2[:, par, w_, c * 128:(c + 1) * 128], identity=ident)
                    return ins
                sc.op("pe", ftr, [res("QK%d_%d" % (w_, par)) for w_ in range(4)], [res("pb1")])
                sc.op("dve", lambda e: e.tensor_copy(out=KDT2[:, par], in_=pb1[:, 4:8, :]),
                      [res("pb1")], [res("KDT%d" % par)])
                sc.op("dve", lambda e: e.tensor_copy(out=QEz2[0:64, par, 0, :, :], in_=pb1[0:64, 0:4, :]),
                      [res("pb1"), res("QEz_init")], [res("QEz0_%d" % par)])
                sc.op("dve", lambda e: e.tensor_copy(out=QEz2[64:128, par, 1, :, :], in_=pb1[64:128, 0:4, :]),
                      [res("pb1"), res("QEz_init")], [res("QEz1_%d" % par)])

            def H2a(t):
                par = t % 2
                xp = t % 3
                mixT = mixT2[:, par]
                ef = ef2[:, par, :]
                KDT = KDT2[:, par]
                QEz = QEz2[:, par]
                v_bf = vbf2[:, par, :]
                rq = [res("KDT%d" % par), res("QEz0_%d" % par), res("QEz1_%d" % par)]
                rvb_ = res("v_bf%d" % par)
                zgb, rzg = bank(2)
                proj_tok(xp, OZG, 512, zgb, rzg)
                sc.op("act", lambda e: e.activation(out=eg, in_=zgb, func=AF.Exp, scale=-1.0), [rzg], [res("eg")])
                sc.op("act", lambda e: e.activation(out=eg, in_=eg, func=AF.Ln, bias=1.0), [res("eg")], [res("eg")])
                sc.op("act", lambda e: e.activation(out=eg, in_=eg, func=AF.Exp, scale=-1.0), [res("eg")], [res("eg")])
                sc.op("dve", lambda e: e.tensor_tensor(out=eg, in0=zgb, in1=eg, op=ALU.mult),
                      [rzg, res("eg")], [res("eg")])
                sc.op("pool", lambda e: e.tensor_tensor(out=eg, in0=eg, in1=gain, op=ALU.mult),
                      [res("eg")], [res("eg")])
                for d_ in range(2):
                    b_, rb_ = bank(2 + d_)

                    def fsc(e, b_=b_, d_=d_):
                        ins = None
                        for h in range(4):
                            c = h // 2
                            ins = e.matmul(b_[:, h * 128:(h + 1) * 128], lhsT=KDT[:, d_ * 2 + c, :],
                                           rhs=QEz[:, h % 2, d_ * 2 + c, :], start=True, stop=True)
                        return ins
                    sc.op("pe", fsc, rq, [rb_])
                    msk = tri[:, d_ * 128:(d_ + 1) * 128]
                    sc.op("dve", lambda e, b_=b_, d_=d_, msk=msk: e.tensor_tensor(
                        out=scm[:, d_, :].rearrange("p (h i) -> p h i", h=4),
                        in0=b_.rearrange("p (h i) -> p h i", h=4),
                        in1=msk.unsqueeze(1).to_broadcast([128, 4, 128]), op=ALU.mult),
                        [rb_], [res("scm%d" % d_)], cost=0.7)
                ob, rob = bank(2)

                def fo(e):
                    ins = None
                    for h in range(4):
                        c = h // 2
                        o_ = ob[:, h * 128:(h + 1) * 128]
                        e.matmul(o_, lhsT=scm[:, 0, h * 128:(h + 1) * 128], rhs=v_bf[:, h * 128:(h + 1) * 128],
                                 start=True, stop=False)
                        e.matmul(o_, lhsT=scm[:, 1, h * 128:(h + 1) * 128], rhs=v_bf[:, h * 128:(h + 1) * 128],
                                 start=False, stop=False)
                        e.matmul(o_, lhsT=QEz[:, h % 2, 0 + c, :], rhs=Sf_bf[:, c * 128:(c + 1) * 128],
                                 start=False, stop=False)
                        ins = e.matmul(o_, lhsT=QEz[:, h % 2, 2 + c, :],
                                       rhs=Sbs[:, t, c * 128:(c + 1) * 128], start=False, stop=True)
                    return ins
                sc.op("pe", fo, [res("scm0"), res("scm1"), rvb_, rq[1], rq[2], res("Sf_bf"), res("Sbs")], [rob], cost=1.1)
                state_update(QK2[:, par, 2, :], res("QK2_%d" % par), v_bf, rvb_, Sf, res("Sf"),
                             sdec2[:, par, 0:2], res("sdec%d" % par), 3)
                sc.op("pool", lambda e: e.tensor_copy(out=Sf_bf, in_=Sf), [res("Sf")], [res("Sf_bf")])
                for h in range(4):
                    sc.op("act", lambda e, h=h: e.activation(out=junk, in_=ob[:, h * 128:(h + 1) * 128],
                                                             func=AF.Square, accum_out=ss[:, h:h + 1]),
                          [rob], [res("ss%d" % h), res("junk")])
                sc.op("act", lambda e: e.activation(out=rstd, in_=ss, func=AF.Ln, scale=1.0 / 128.0, bias=RMS_EPS),
                      [res("ss%d" % h) for h in range(4)], [res("rstd")])
                sc.op("act", lambda e: e.activation(out=rstd, in_=rstd, func=AF.Exp, scale=-0.5),
                      [res("rstd")], [res("rstd")])
                for h in range(4):
                    sc.op("dve", lambda e, h=h: e.scalar_tensor_tensor(
                        out=yg[:, h * 128:(h + 1) * 128], in0=ob[:, h * 128:(h + 1) * 128],
                        scalar=rstd[:, h:h + 1], in1=eg[:, h * 128:(h + 1) * 128], op0=ALU.mult, op1=ALU.mult),
                        [rob, res("rstd"), res("eg")], [res("yg")])

                def ftry(e):
                    for h in range(4):
                        ins = e.transpose(out=ygT_ps[:, h, :], in_=yg[:, h * 128:(h + 1) * 128], identity=ident)
                    return ins
                sc.op("pe", ftry, [res("yg")], [res("m_ygT")])
                sc.op("act", lambda e: e.activation(out=mixT[:, 0:4, :], in_=ygT_ps, func=AF.Copy),
                      [res("m_ygT")], [res("mixT_a%d" % par)])

            def H2b(t):
                par = t % 2
                xsl = xslot[t]
                mixT = mixT2[:, par]
                ef = ef2[:, par, :]
                xp = t % 3
                zfb, rzf = bank(4)
                for g in range(4):
                    proj_feat(xp, OZF + g * 128, 128, zfb[:, g * 128:(g + 1) * 128], rzf)
                sc.op("act", lambda e: e.activation(out=ef, in_=zfb, func=AF.Exp, scale=-1.0), [rzf], [res("ef%d" % par)])
                sc.op("act", lambda e: e.activation(out=ef, in_=ef, func=AF.Ln, bias=1.0), [res("ef%d" % par)], [res("ef%d" % par)])
                sc.op("act", lambda e: e.activation(out=ef, in_=ef, func=AF.Exp, scale=-1.0), [res("ef%d" % par)], [res("ef%d" % par)])
                sc.op("dve", lambda e: e.tensor_tensor(out=ef, in0=zfb, in1=ef, op=ALU.mult),
                      [rzf, res("ef%d" % par)], [res("ef%d" % par)])
                sc.op("pool", lambda e: e.tensor_tensor(
                    out=mixT[:, 4:8, :].rearrange("p g (a k) -> p g a k", k=N2),
                    in0=ef.rearrange("p (g a k) -> p g a k", g=4, k=N2),
                    in1=FNp[:, :, :, t * Q:(t + 1) * Q].rearrange("p g k a -> p g a k"), op=ALU.mult),
                    [res("ef%d" % par)] + r_uf, [res("mixT_b%d" % par)], cost=1.5)
                for hf in range(2):
                    b_, rb_ = bank(4)

                    def fy(e, b_=b_, hf=hf):
                        for ec in range(8):
                            ins = e.matmul(b_, lhsT=mixT[:, ec, :], rhs=wout_bf[:, ec, hf * 512:(hf + 1) * 512],
                                           start=(ec == 0), stop=(ec == 7))
                        return ins
                    sc.op("pe", fy, [res("mixT_a%d" % par), res("mixT_b%d" % par)], [rb_], cost=1.8)
                    sc.op("dve", lambda e, b_=b_, hf=hf: e.scalar_tensor_tensor(
                        out=rt[:, hf * 512:(hf + 1) * 512], in0=xs[xsl][:, hf * 512:(hf + 1) * 512], scalar=ALPHA,
                        in1=b_, op0=ALU.mult, op1=ALU.add), [r_x[xsl], rb_], [res("rt%d" % hf)], cost=0.7)
                    sc.op("dve", lambda e, hf=hf: e.bn_stats(out=bst[:, hf, :], in_=rt[:, hf * 512:(hf + 1) * 512]),
                          [res("rt%d" % hf)], [res("bst%d" % hf)], cost=0.7)
                sc.op("dve", lambda e: e.bn_aggr(out=mv, in_=bst), [res("bst0"), res("bst1")], [res("mv")])
                sc.op("act", lambda e: e.activation(out=lnr[:, 0:1], in_=mv[:, 1:2], func=AF.Ln, bias=LN_EPS),
                      [res("mv")], [res("lnr0")])
                sc.op("act", lambda e: e.activation(out=lnr[:, 0:1], in_=lnr[:, 0:1], func=AF.Exp, scale=-0.5),
                      [res("lnr0")], [res("lnr0")])
                sc.op("dve", lambda e: e.scalar_tensor_tensor(out=lnr[:, 1:2], in0=mv[:, 0:1], scalar=-1.0,
                                                              in1=lnr[:, 0:1], op0=ALU.mult, op1=ALU.mult),
                      [res("mv"), res("lnr0")], [res("lnr1")])
                osl = ocount[0] % 2
                ocount[0] += 1
                sc.op("act", lambda e: e.activation(out=ot[osl], in_=rt, func=AF.Identity,
                                                    scale=lnr[:, 0:1], bias=lnr[:, 1:2]),
                      [res("rt0"), res("rt1"), res("lnr0"), res("lnr1")], [r_ot[osl]], cost=1.25)
                sc.op("pool", lambda e: e.tensor_tensor(out=ot[osl], in0=ot[osl], in1=lng, op=ALU.mult),
                      [r_ot[osl]], [r_ot[osl]], cost=2.35)
                sc.op("pool", lambda e: e.tensor_tensor(out=ot[osl], in0=ot[osl], in1=lnb, op=ALU.add),
                      [r_ot[osl]], [r_ot[osl]], cost=2.35)
                dma("sp", out_d[base + t * 128:base + (t + 1) * 128, :], ot[osl], [r_ot[osl]], [], "d_o%d" % osl)

            H1(0)
            if NT > 1:
                sc.zipper(sc.record(H1, 1), sc.record(H2a, 0))
            else:
                H2a(0)
            for t in range(NT):
                lists = []
                pr = []
                if t + 2 < NT:
                    lists.append(sc.record(H1, t + 2))
                    pr.append(PRIO2[0])
                if t + 1 < NT:
                    lists.append(sc.record(H2a, t + 1))
                    pr.append(PRIO2[1])
                lists.append(sc.record(H2b, t))
                pr.append(PRIO2[2])
                sc.zipper(*lists, prio=pr, sched=SCHED2)

        sc.wait_all("sp", [("d_o0", sc.cnt.get("d_o0", 0)), ("d_o1", sc.cnt.get("d_o1", 0))])

        for key in sc.cnt:
            sc.sems[key] = es.enter_context(nc.semaphore("m_" + key))
        with nc.Block() as block:
            @block.tensor
            def _(e):
                sc.emit("pe", e)

            @block.scalar
            def _(e):
                sc.emit("act", e)

            @block.vector
            def _(e):
                sc.emit("dve", e)

            @block.gpsimd
            def _(e):
                sc.emit("pool", e)

            @block.sync
            def _(e):
                sc.emit("sp", e)
    return nc


def _shared_inputs(w_in, w_gate_up_fwd, b_gate_fwd, w_gate_up_bwd, b_gate_bwd, gla_norm_g,
                   w_fnet, w_out, ln_g, ln_b, S):
    f = np.float32
    wup = np.zeros((33, 512), f)
    wup[0:16, 0:256] = np.asarray(w_gate_up_fwd, f)[0]
    wup[16:32, 256:512] = np.asarray(w_gate_up_bwd, f)[0]
    wup[32, 0:256] = np.asarray(b_gate_fwd, f)[0]
    wup[32, 256:512] = np.asarray(b_gate_bwd, f)[0]
    wf = np.ascontiguousarray(np.transpose(np.asarray(w_fnet, f)[0], (1, 0, 2)).reshape(128, 512))
    d = dict(
        w_in=np.ascontiguousarray(np.asarray(w_in, f)[0]),
        w_out=np.ascontiguousarray(np.asarray(w_out, f)[0]),
        wup=wup,
        gain=np.ascontiguousarray(np.asarray(gla_norm_g, f)[0].reshape(1, 512)),
        w_fnet=wf,
        ln_g=np.ascontiguousarray(np.asarray(ln_g, f)[0].reshape(1, D)),
        ln_b=np.ascontiguousarray(np.asarray(ln_b, f)[0].reshape(1, D)),
    )
    d.update(_tables(S))
    return d


def kernel(x, w_in, w_gate_up_fwd, b_gate_fwd, w_gate_up_bwd, b_gate_bwd, gla_norm_g,
           w_fnet, w_out, ln_g, ln_b):
    x = np.asarray(x, np.float32)
    B, S, _ = x.shape
    nseq = B // NCORES
    shared = _shared_inputs(w_in, w_gate_up_fwd, b_gate_fwd, w_gate_up_bwd, b_gate_bwd, gla_norm_g,
                            w_fnet, w_out, ln_g, ln_b, S)
    nc = build(S, nseq)
    in_maps = []
    for c in range(NCORES):
        m = dict(shared)
        m["x"] = np.ascontiguousarray(x[c * nseq:(c + 1) * nseq].reshape(nseq * S, D))
        in_maps.append(m)
    res = run_bass_kernel_spmd(nc, in_maps, core_ids=list(range(NCORES)))
    outs = [np.asarray(r["out"], np.float32).reshape(nseq, S, D) for r in res.results]
    return np.concatenate(outs, axis=0)
```

```python
import math
from contextlib import ExitStack

import numpy as np
import ml_dtypes

import concourse.bass as bass
import concourse.mybir as mybir
from concourse.bass_utils import run_bass_kernel_spmd

F32 = mybir.dt.float32
BF16 = mybir.dt.bfloat16
AF = mybir.ActivationFunctionType
ALU = mybir.AluOpType

D = 1024
NCORES = 8
E_IN = 2592
OQ, OK_, OV, OGF, OGB, OZG, OU, OZF = 0, 256, 512, 1024, 1040, 1056, 1568, 2080
ALPHA = 2.0 ** 0.25
LN_EPS = 1e-5
RMS_EPS = 1e-6
DVE_RUNS = True
PRIO1 = (0.0, 0.0, 0.0)
PRIO2 = (0.0, 0.0, 0.0)
SCHED2 = False
LIST_SCHED = True
SAME_SYNC = True


class Res:
    __slots__ = ("w", "r", "name", "excl")

    def __init__(self, name="", excl=False):
        self.w = None
        self.r = {}
        self.name = name
        self.excl = excl


class Sched:
    ENGS = ("pe", "act", "dve", "pool", "sp")

    def __init__(self):
        self.ops = {e: [] for e in self.ENGS}
        self.cnt = {}
        self.waited = {e: {} for e in self.ENGS}
        self.sems = {}
        self.rec = None
        self.vfree = {}
        self.vw = {}
        self.vr = {}

    def record(self, f, *a):
        assert self.rec is None
        self.rec = []
        f(*a)
        r, self.rec = self.rec, None
        return r

    def zipper(self, *lists, prio=None, sched=True):
        idx = [0] * len(lists)
        while True:
            best, bi = None, -1
            for i, l in enumerate(lists):
                if idx[i] < len(l):
                    eng, fn, reads, writes, dma, cost = l[idx[i]]
                    if LIST_SCHED and sched:
                        key = (self._est(eng, reads, writes, dma) - (prio[i] if prio else 0.0),
                               (idx[i] + 0.5) / len(l))
                    else:
                        key = ((idx[i] + 0.5) / len(l) + (prio[i] if prio else 0.0),)
                    if best is None or key < best:
                        best, bi = key, i
            if bi < 0:
                break
            self.op(*lists[bi][idx[bi]])
            idx[bi] += 1

    DEF_COST = {"pe": 0.5, "act": 0.6, "dve": 0.45, "pool": 1.3, "sp": 0.1}
    HOP = 0.35

    def _est(self, eng, reads, writes, dma):
        t = self.vfree.get(eng, 0.0)
        key = dma if dma is not None else eng
        for r in reads:
            w = self.vw.get(id(r))
            if w is not None:
                t = max(t, w[1] + (self.HOP if w[0] != eng else 0.0))
        for w_ in writes:
            w = self.vw.get(id(w_))
            if w is not None:
                t = max(t, w[1] + (self.HOP if w[0] != eng else 0.0))
            for k, v in self.vr.get(id(w_), {}).items():
                t = max(t, v + (self.HOP if k != eng else 0.0))
        return t

    def _commit(self, eng, reads, writes, dma, cost):
        t0 = self._est(eng, reads, writes, dma)
        if dma is not None:
            self.vfree[eng] = t0 + 0.1
            t1 = t0 + (cost if cost is not None else 3.0)
            who = "dma"
        else:
            t1 = t0 + (cost if cost is not None else self.DEF_COST[eng])
            self.vfree[eng] = t1
            who = eng
        for r in reads:
            d = self.vr.setdefault(id(r), {})
            d[who] = max(d.get(who, 0.0), t1)
        for w_ in writes:
            self.vw[id(w_)] = (who, t1)
            self.vr[id(w_)] = {}

    def op(self, eng, fn, reads=(), writes=(), dma=None, cost=None):
        if self.rec is not None:
            self.rec.append((eng, fn, tuple(reads), tuple(writes), dma, cost))
            return None
        self._commit(eng, reads, writes, dma, cost)
        deps = {}

        def add(tok):
            if tok is None:
                return
            k, v = tok
            if deps.get(k, 0) < v:
                deps[k] = v

        own = "e_" + eng
        for r in reads:
            add(r.w)
            if r.excl:
                for k, v in r.r.items():
                    if k != own:
                        add((k, v))
        for w in writes:
            add(w.w)
            for k, v in w.r.items():
                add((k, v))
        own = "e_" + eng
        waits = []
        for k, v in deps.items():
            if k == own and (eng == "pe" or not SAME_SYNC):
                continue
            if self.waited[eng].get(k, 0) < v:
                self.waited[eng][k] = v
                waits.append((k, v))
        if dma is None:
            key, inc = own, 1
        else:
            key, inc = dma, 16
        val = self.cnt.get(key, 0) + inc
        self.cnt[key] = val
        tok = (key, val)
        self.ops[eng].append((waits, fn, key, inc))
        for r in reads:
            if r.r.get(key, 0) < val:
                r.r[key] = val
        for w in writes:
            w.w = tok
            w.r = {}
        return tok

    def wait_all(self, eng, toks):
        waits = []
        for k, v in toks:
            if self.waited[eng].get(k, 0) < v:
                self.waited[eng][k] = v
                waits.append((k, v))
        if waits:
            self.ops[eng].append((waits, None, None, 0))

    def emit(self, eng, e):
        for waits, fn, key, inc in self.ops[eng]:
            for k, v in waits:
                e.wait_ge(self.sems[k], v)
            if fn is not None:
                ins = fn(e)
                ins.then_inc(self.sems[key], inc)


def _tables(S):
    N1 = 128
    N2 = S // N1
    Q = 128 // N2
    bf = ml_dtypes.bfloat16
    scale = 1.0 / math.sqrt(S * 128.0)
    c = np.arange(128)
    ang = -2.0 * np.pi * np.outer(c, c) / 128.0
    FC = np.concatenate([np.cos(ang) * scale, np.sin(ang) * scale], axis=1)
    n1 = np.arange(N1)
    k1 = np.arange(N1)
    T = np.zeros((N2, 128, 2, 256), np.float64)
    for n2 in range(N2):
        a = -2.0 * np.pi * np.outer(N2 * n1 + n2, k1) / S
        tre, tim = np.cos(a), np.sin(a)
        perm = np.array([N2 * q + kr for kr in range(N2) for q in range(Q)])
        tre, tim = tre[:, perm], tim[:, perm]
        T[n2, :, 0, :128] = tre
        T[n2, :, 0, 128:] = tim
        T[n2, :, 1, :128] = -tim
        T[n2, :, 1, 128:] = tre
    BD = np.zeros((128, 2, 128), np.float64)
    for q in range(Q):
        for n2 in range(N2):
            for k2 in range(N2):
                a = -2.0 * np.pi * n2 * k2 / N2
                BD[n2 * Q + q, 0, q + Q * k2] = np.cos(a)
                BD[n2 * Q + q, 1, q + Q * k2] = -np.sin(a)
    j = np.arange(128)[:, None]
    i = np.arange(128)[None, :]
    tri = np.concatenate([(j <= i), (j >= i)], axis=1).astype(np.float32)
    ident = np.eye(128, dtype=np.float32).astype(bf)
    return dict(
        c_fc=FC.astype(np.float32).astype(bf),
        c_tt=T.reshape(N2, 128, 512).astype(np.float32).astype(bf),
        c_bd=BD.reshape(128, 256).astype(np.float32).astype(bf),
        c_tri=tri,
        c_id=ident,
    )


def build(S=4096, NSEQ=2):
    NT = S // 128
    N2 = S // 128
    Q = 128 // N2
    KRB = min(4, N2)
    nc = bass.Bass("TRN2", target_bir_lowering=False)
    TOK = NSEQ * S

    def din(name, shape, dt=F32):
        return nc.dram_tensor(name, shape, dt, kind="ExternalInput").ap()

    x_d = din("x", [TOK, D])
    win_d = din("w_in", [D, E_IN])
    wout_d = din("w_out", [D, D])
    wup_d = din("wup", [33, 512])
    gain_d = din("gain", [1, 512])
    wf_d = din("w_fnet", [128, 512])
    lng_d = din("ln_g", [1, D])
    lnb_d = din("ln_b", [1, D])
    fc_d = din("c_fc", [128, 256], BF16)
    tt_d = din("c_tt", [N2, 128, 512], BF16)
    bd_d = din("c_bd", [128, 256], BF16)
    tri_d = din("c_tri", [128, 256])
    id_d = din("c_id", [128, 128], BF16)
    out_d = nc.dram_tensor("out", [TOK, D], F32, kind="ExternalOutput").ap()

    sc = Sched()
    es = ExitStack()
    with es:
        def sb(name, shape, dt):
            return es.enter_context(nc.sbuf_tensor("s_" + name, shape, dt))[:]

        def ps(name, shape, dt):
            return es.enter_context(nc.psum_tensor("p_" + name, shape, dt))[:]

        win_bf = sb("win_bf", [128, 8, E_IN], BF16)
        wout_bf = sb("wout_bf", [128, 8, D], BF16)
        wup_bf = sb("wup_bf", [33, 512], BF16)
        tri = sb("tri", [128, 256], F32)
        ident = sb("ident", [128, 128], BF16)
        ones = sb("ones", [128, 1], F32)
        gain = sb("gain", [128, 512], F32)
        lng = sb("lng", [128, D], F32)
        lnb = sb("lnb", [128, D], F32)
        wf_bf = sb("wf_bf", [128, 512], BF16)
        fc = sb("fc", [128, 256], BF16)
        bd = sb("bd", [128, 256], BF16)
        UF = sb("UF", [128, 4, S], BF16)
        Sbs = sb("Sbs", [128, NT, 256], BF16)
        Zg = sb("Zg", [128, 2 * N2 * 128], BF16)
        Sf = sb("Sf", [128, 256], F32)
        Sf_bf = sb("Sf_bf", [128, 256], BF16)
        Sbst = sb("Sbst", [128, 256], F32)
        stmp = sb("stmp", [128, 256], F32)
        NXS = 4
        big = sb("big", [128, NXS + 3, D], F32)
        xb = sb("xb", [128, D], BF16)
        xT2 = sb("xT", [128, 3, 8, 128], BF16)
        gT2 = sb("gT", [33, 2, 128], BF16)
        ksb2 = sb("ksb", [128, 2, 256], F32)
        e1 = sb("e1", [128, 512], F32)
        sdec2 = sb("sdec", [128, 2, 4], F32)
        QK2 = sb("QK", [128, 2, 4, 256], BF16)
        vbf2 = sb("v_bf", [128, 3, 512], BF16)
        KDT2 = sb("KDT", [128, 2, 4, 128], BF16)
        QEz2 = sb("QEz", [128, 2, 2, 4, 128], BF16)
        ss = sb("ss", [128, 4], F32)
        rstd = sb("rstd", [128, 4], F32)
        if 2 * N2 * 128 >= 6656:
            ef2 = Zg[:, 0:2048].bitcast(F32).rearrange("p (a b) -> p a b", a=2)
            eg = Zg[:, 2048:3072].bitcast(F32)
            scm = Zg[:, 3072:4096].rearrange("p (a b) -> p a b", a=2)
            mixT2 = Zg[:, 4096:6144].rearrange("p (a b c) -> p a b c", a=2, b=8)
            yg = Zg[:, 6144:6656]
        else:
            ef2 = sb("ef", [128, 2, 512], F32)
            eg = sb("eg", [128, 512], F32)
            scm = sb("scm", [128, 2, 512], BF16)
            mixT2 = sb("mixT", [128, 2, 8, 128], BF16)
            yg = sb("yg", [128, 512], BF16)
        bst = sb("bst", [128, 2, 6], F32)
        mv = sb("mv", [128, 2], F32)
        lnr = sb("lnr", [128, 2], F32)
        junk = sb("junk", [128, 128], F32)
        EqEk = sb("EqEk", [128, 2, 512], F32)
        Eq = EqEk[:, 0, :]
        Ek = EqEk[:, 1, :]
        tts = EqEk.rearrange("p a b -> p (a b)").bitcast(BF16).rearrange("p (s n) -> p s n", s=4)
        tts1 = xT2.rearrange("p a b c -> p (a b c)")[:, 0:2048].rearrange("p (s n) -> p s n", s=4)
        ttsl = [tts, tts1]
        NTD = 2
        Vs = e1.bitcast(BF16).rearrange("p (s n) -> p s n", s=4)
        Ys = QK2.rearrange("p a b c -> p (a b c)").rearrange("p (s n) -> p s n", s=4)

        NROT = 5
        rot = [ps("rot%d" % i, [128, 512], F32) for i in range(NROT)]
        misc = ps("misc", [128, 512], F32)
        pb0 = ps("pb0", [128, 8, 128], BF16)
        pb1 = ps("pb1", [128, 8, 128], BF16)
        ygT_ps = misc[:, 256:512].bitcast(BF16).rearrange("p (h i) -> p h i", h=4)

        R = {}

        def res(name):
            if name.startswith("m_"):
                name = "misc"
            if name not in R:
                R[name] = Res(name, excl=(name == "misc" or name.startswith("rot") or name.startswith("pb")))
            return R[name]

        rot_i = [0]

        def nextrot():
            i = rot_i[0] % NROT
            rot_i[0] += 1
            return rot[i], res("rot%d" % i)

        def bank(i):
            return rot[i], res("rot%d" % i)

        def full_barrier():
            toks = [(k, v) for k, v in sc.cnt.items()]
            for eng in Sched.ENGS:
                sc.wait_all(eng, toks)

        def dma(eng, out, in_, reads, writes, key):
            return sc.op(eng, lambda e: e.dma_start(out=out, in_=in_), reads, writes, dma=key)

        bigflat = big.rearrange("p a b -> p (a b)")
        HW = E_IN // 2
        stage = [bigflat[:, i * HW:(i + 1) * HW] for i in range(4)]
        r_stage = [res("stage%d" % i) for i in range(4)]
        si = 0
        cast_eng = ["act", "dve", "act", "dve"]

        def cast(eng, o_, i_, reads, writes):
            if eng == "act":
                sc.op("act", lambda e: e.activation(out=o_, in_=i_, func=AF.Copy), reads, writes)
            else:
                sc.op(eng, lambda e: e.tensor_copy(out=o_, in_=i_), reads, writes)

        for kc in range(8):
            for hf in range(2):
                s_ = si % 4
                dma("sp", stage[s_], win_d[kc * 128:(kc + 1) * 128, hf * HW:(hf + 1) * HW],
                    [], [r_stage[s_]], "d_stage%d" % s_)
                cast(cast_eng[s_], win_bf[:, kc, hf * HW:(hf + 1) * HW], stage[s_],
                     [r_stage[s_]], [res("win_bf_" + cast_eng[s_])])
                si += 1
        for kc in range(8):
            s_ = si % 4
            dma("sp", stage[s_][:, 0:D], wout_d[kc * 128:(kc + 1) * 128, :],
                [], [r_stage[s_]], "d_stage%d" % s_)
            cast(cast_eng[s_], wout_bf[:, kc, :], stage[s_][:, 0:D],
                 [r_stage[s_]], [res("wout_bf_" + cast_eng[s_])])
            si += 1
        cst = res("consts")
        dma("pool", e1[0:33, :], wup_d, [], [res("wup_f")], "d_c0")
        dma("pool", Eq, wf_d, [], [res("wf_f")], "d_c1")
        dma("pool", tri, tri_d, [], [cst], "d_c2")
        dma("pool", ident, id_d, [], [cst], "d_c2")
        dma("pool", fc, fc_d, [], [cst], "d_c2")
        dma("pool", bd, bd_d, [], [cst], "d_c2")
        dma("pool", gain, gain_d.partition_broadcast(128), [], [cst], "d_c2")
        dma("pool", lng, lng_d.partition_broadcast(128), [], [cst], "d_c2")
        dma("pool", lnb, lnb_d.partition_broadcast(128), [], [cst], "d_c2")
        sc.op("act", lambda e: e.activation(out=wup_bf, in_=e1[0:33, :], func=AF.Copy),
              [res("wup_f")], [res("wup_bf")])
        sc.op("act", lambda e: e.activation(out=wf_bf, in_=Eq, func=AF.Copy),
              [res("wf_f")], [res("wf_bf")])
        sc.op("dve", lambda e: e.memset(ones, 1.0), [], [cst])
        sc.op("dve", lambda e: e.memset(gT2[32:33, :, :], 1.0), [], [res("gT_ones")])
        sc.op("dve", lambda e: e.memset(QEz2, 0.0), [], [res("QEz_init")])
        init_toks = [(k, v) for k, v in sc.cnt.items()]
        for eng in Sched.ENGS:
            sc.wait_all(eng, init_toks)

        r_x = [res("x%d" % i) for i in range(NXS)]
        xs = [big[:, i, :] for i in range(NXS)]
        ot = [big[:, NXS + i, :] for i in range(2)]
        r_ot = [res("ot0"), res("ot1")]
        rt = big[:, NXS + 2, :]
        xcount = [0]
        ocount = [0]

        def load_x(row0):
            s_ = xcount[0] % NXS
            xcount[0] += 1
            dma("sp", xs[s_], x_d[row0:row0 + 128, :], [], [r_x[s_]], "d_x%d" % s_)
            return s_

        def make_xT(s_, xp):
            par = xp
            sc.op("pool", lambda e: e.tensor_copy(out=xb[:, 0:512], in_=xs[s_][:, 0:512]),
                  [r_x[s_]], [res("xb0")], cost=1.85)
            sc.op("dve",
                  lambda e: e.tensor_copy(out=xb[:, 512:1024], in_=xs[s_][:, 512:1024]),
                  [r_x[s_]], [res("xb1")])

            def f(e):
                for kc in range(8):
                    ins = e.transpose(out=pb0[:, kc, :], in_=xb[:, kc * 128:(kc + 1) * 128],
                                      identity=ident)
                return ins
            sc.op("pe", f, [res("xb0"), res("xb1")], [res("pb0")], cost=0.6)
            sc.op("dve", lambda e: e.tensor_copy(out=xT2[:, par], in_=pb0), [res("pb0")], [res("xT%d" % par)], cost=0.7)

        def proj_tok(par, c0, n, dst, rdst):
            COSTP = 8 * n / 2400.0 + 0.05
            def f(e):
                for kc in range(8):
                    ins = e.matmul(dst, lhsT=xT2[:, par, kc, :], rhs=win_bf[:, kc, c0:c0 + n],
                                   start=(kc == 0), stop=(kc == 7))
                return ins
            sc.op("pe", f, [res("xT%d" % par)], [rdst], cost=COSTP)

        def proj_feat(par, c0, m, dst, rdst):
            COSTP = 8 * 0.06 + 0.05
            def f(e):
                for kc in range(8):
                    ins = e.matmul(dst, lhsT=win_bf[:, kc, c0:c0 + m], rhs=xT2[:, par, kc, :],
                                   start=(kc == 0), stop=(kc == 7))
                return ins
            sc.op("pe", f, [res("xT%d" % par)], [rdst], cost=COSTP)

        def gates(par, ncols, col0, bk):
            zb, rzb = bank(bk)
            sc.op("pe", lambda e: e.matmul(zb[:, 0:ncols], lhsT=gT2[0:33, par, :],
                                           rhs=wup_bf[0:33, col0:col0 + ncols], start=True, stop=True),
                  [res("gT%d" % par), res("gT_ones")], [rzb])
            sc.op("act", lambda e: e.activation(out=e1[:, 0:ncols], in_=zb[:, 0:ncols], func=AF.Exp,
                                                scale=-1.0), [rzb], [res("e1")])
            sc.op("act", lambda e: e.activation(out=e1[:, 0:ncols], in_=e1[:, 0:ncols], func=AF.Ln,
                                                bias=1.0), [res("e1")], [res("e1")])

        def state_update(kd_ap, rkd, vb_ap, rvb, S_ap, rS, sd_ap, rsd, bk):
            ub, rub = bank(bk)
            u4 = ub.rearrange("p (c h v) -> p c h v", c=2, h=2)

            def f2(e):
                ins = None
                for c in range(2):
                    for hh in range(2):
                        h = 2 * c + hh
                        ins = e.matmul(u4[:, c, hh, :], lhsT=kd_ap[:, c * 128:(c + 1) * 128],
                                       rhs=vb_ap[:, h * 128:(h + 1) * 128], start=True, stop=True)
                return ins
            sc.op("pe", f2, [rkd, rvb], [rub])
            Sv = S_ap.rearrange("p (c v) -> p c v", c=2)
            tv = stmp.rearrange("p (c v) -> p c v", c=2)
            for hh in range(2):
                p0 = hh * 64
                sc.op("dve", lambda e, p0=p0, hh=hh: e.tensor_tensor(
                    out=tv[p0:p0 + 64], in0=Sv[p0:p0 + 64], in1=u4[p0:p0 + 64, :, hh, :], op=ALU.add),
                    [rS, rub], [res("stmp%d" % hh)])
            for c in range(2):
                sc.op("dve", lambda e, c=c: e.tensor_scalar(
                    out=Sv[:, c, :], in0=tv[:, c, :], scalar1=sd_ap[:, c:c + 1],
                    scalar2=None, op0=ALU.mult), [res("stmp0"), res("stmp1"), rsd], [rS])

        UFv = UF.rearrange("p g (b a) -> p g b a", b=N2)
        FNv = UF
        ZN = 2 * N2 * 128
        Zg1 = bigflat.bitcast(BF16)[:, 0:ZN]
        Zv2 = [z_.rearrange("p (r k n q) -> p r k n q", r=2, k=N2, q=Q) for z_ in (Zg, Zg1)]
        fbank = [[bank(0), bank(1), bank(2), bank(3)],
                 [bank(4), (misc, res("misc")),
                  (pb0.rearrange("p a b -> p (a b)").bitcast(F32), res("pb0")),
                  (pb1.rearrange("p a b -> p (a b)").bitcast(F32), res("pb1"))]]
        FNp = UF.rearrange("p g (k m) -> p g k m", k=N2)
        r_uf = [res("uf%d" % g) for g in range(4)]

        for s in range(NSEQ):
            base = s * S
            sc.op("dve", lambda e: e.memset(Sbst, 0.0), [], [res("Sbst")])
            sc.op("dve", lambda e: e.memset(Sf, 0.0), [], [res("Sf")])
            sc.op("dve", lambda e: e.memset(Sf_bf, 0.0), [], [res("Sf_bf")])
            order = list(range(NT - 1, -1, -1))
            xslot = {}

            def G1(oi):
                t = order[oi]
                par = oi % 2
                xp = oi % 3
                if oi == 0:
                    xslot[0] = load_x(base + order[0] * 128)
                if oi + 1 < NT:
                    xslot[oi + 1] = load_x(base + order[oi + 1] * 128)
                make_xT(xslot[oi], xp)
                kb, rkb = bank(0)
                proj_feat(xp, OGF, 32, misc[0:32, 0:128], res("m_g"))
                proj_tok(xp, OK_, 256, kb[:, 0:256], rkb)
                vb, rvb = bank(1)
                proj_tok(xp, OV, 512, vb, rvb)
                ub, rub = bank(2)
                for g in range(4):
                    proj_feat(xp, OU + g * 128, 128, ub[:, g * 128:(g + 1) * 128], rub)
                sc.op("act", lambda e: e.activation(out=gT2[0:32, par, :], in_=misc[0:32, 0:128], func=AF.Copy),
                      [res("m_g")], [res("gT%d" % par)])
                sc.op("act", lambda e: e.activation(out=ksb2[:, par, :], in_=kb[:, 0:256], func=AF.Copy),
                      [rkb], [res("ksb%d" % par)])
                sc.op("act", lambda e: e.activation(out=vbf2[:, xp, :], in_=vb, func=AF.Copy),
                      [rvb], [res("v_bf%d" % xp)])
                src_ = ub.rearrange("p (g a b) -> p g a b", g=4, b=N2)
                dst_ = UFv[:, :, :, t * Q:(t + 1) * Q].rearrange("p g b a -> p g a b")
                sc.op("act", lambda e: e.activation(out=dst_, in_=src_, func=AF.Copy), [rub], r_uf, cost=1.5)

            def G2a(oi):
                t = order[oi]
                par = oi % 2
                gates(par, 256, 256, 3)
                cb, rcb = bank(3)
                sc.op("pe", lambda e: e.matmul(cb[:, 0:256], lhsT=tri[:, 128:256], rhs=e1[:, 0:256],
                                               start=True, stop=True), [res("e1")], [rcb])

                def ftot(e):
                    for c in range(2):
                        ins = e.matmul(misc[:, 128 + c:129 + c], lhsT=e1[:, c * 128:(c + 1) * 128],
                                       rhs=ones[:, 0:1], start=True, stop=True)
                    return ins
                sc.op("pe", ftot, [res("e1")], [res("m_tot")])
                sc.op("act", lambda e: e.activation(out=Ek[:, 0:256], in_=cb[:, 0:256], func=AF.Exp,
                                                    scale=1.0 / 16.0), [rcb], [res("Ek")])
                sc.op("act", lambda e: e.activation(out=sdec2[:, par, 0:2], in_=misc[:, 128:130], func=AF.Exp,
                                                    scale=-1.0 / 16.0), [res("m_tot")], [res("sdec%d" % par)])
                sc.op("dve", lambda e: e.tensor_tensor(out=QK2[:, par, 3, :], in0=ksb2[:, par, :],
                                                       in1=Ek[:, 0:256], op=ALU.mult),
                      [res("ksb%d" % par), res("Ek")], [res("QK3_%d" % par)])

            def G2b(oi):
                t = order[oi]
                par = oi % 2
                xp = oi % 3
                sc.op("pool", lambda e: e.tensor_copy(out=Sbs[:, t, :], in_=Sbst),
                      [res("Sbst")], [res("Sbs")])
                state_update(QK2[:, par, 3, :], res("QK3_%d" % par), vbf2[:, xp, :], res("v_bf%d" % xp),
                             Sbst, res("Sbst"), sdec2[:, par, 0:2], res("sdec%d" % par), 4)

            G1(0)
            if NT > 1:
                sc.zipper(sc.record(G1, 1), sc.record(G2a, 0))
            else:
                G2a(0)
            for oi in range(NT):
                lists = []
                pr = []
                if oi + 2 < NT:
                    lists.append(sc.record(G1, oi + 2))
                    pr.append(PRIO1[0])
                if oi + 1 < NT:
                    lists.append(sc.record(G2a, oi + 1))
                    pr.append(PRIO1[1])
                lists.append(sc.record(G2b, oi))
                pr.append(PRIO1[2])
                sc.zipper(*lists, prio=pr, sched=True)

            full_barrier()

            def fft_chain(g, c):
                Zc = Zv2[c]
                bA, bB, bT, bS = fbank[c]
                ttl = [0]

                def load_tt(n2):
                    s_ = n2 % 4
                    dma("sp", ttsl[c][:, s_, :], tt_d[n2], [], [res("tts%d_%d" % (c, s_))], "d_tt%d_%d" % (c, s_))

                for n2_ in range(min(NTD, N2)):
                    load_tt(n2_)

                def stepA(n2):
                    if n2 + NTD < N2:
                        load_tt(n2 + NTD)
                    tsl = n2 % 4
                    ab, rab = bA
                    sc.op("pe", lambda e: e.matmul(ab[:, 0:256], lhsT=UFv[:, g, n2, :], rhs=fc,
                                                   start=True, stop=True), [r_uf[g]], [rab])
                    vsl = 2 * c + n2 % 2
                    rv = res("Vs%d" % vsl)
                    sc.op("dve", lambda e: e.tensor_copy(out=Vs[:, vsl, :], in_=ab[:, 0:256]), [rab], [rv])
                    return tsl, vsl

                def stepB(n2, tsl, vsl):
                    rv = res("Vs%d" % vsl)
                    bb, rbb = bB

                    def fB(e):
                        e.matmul(bb[:, 0:256], lhsT=Vs[:, vsl, 0:128], rhs=ttsl[c][:, tsl, 0:256],
                                 start=True, stop=False)
                        return e.matmul(bb[:, 0:256], lhsT=Vs[:, vsl, 128:256], rhs=ttsl[c][:, tsl, 256:512],
                                        start=False, stop=True)
                    sc.op("pe", fB, [rv, res("tts%d_%d" % (c, tsl))], [rbb])
                    dstz = Zc[:, :, :, n2, :]
                    srcz = bb[:, 0:256].rearrange("p (r k q) -> p r k q", r=2, q=Q)
                    if n2 % 2 == 0 or not DVE_RUNS:
                        sc.op("act", lambda e: e.activation(out=dstz, in_=srcz, func=AF.Copy),
                              [rbb], [res("Zg_a%d" % c)])
                    else:
                        sc.op("dve", lambda e: e.tensor_copy(out=dstz, in_=srcz),
                              [rbb], [res("Zg_d%d" % c)])

                pendA = stepA(0)
                for n2 in range(N2):
                    curA = pendA
                    if n2 + 1 < N2:
                        pendA = stepA(n2 + 1)
                    stepB(n2, *curA)

                NP = N2 // 2

                def stepT(i):
                    tb, rtb = bT
                    ysl = 2 * c + i % 2
                    ry = res("Ys%d" % ysl)

                    def fT(e):
                        ins = None
                        for j in range(2):
                            kr = 2 * i + j
                            for r_ in range(2):
                                ins = e.matmul(tb[:, (j * 2 + r_) * 128:(j * 2 + r_ + 1) * 128],
                                               lhsT=Zc[:, r_, kr, :, :].rearrange("p n q -> p (n q)"),
                                               rhs=wf_bf[:, g * 128:(g + 1) * 128], start=True, stop=True)
                        return ins
                    sc.op("pe", fT, [res("Zg_a%d" % c), res("Zg_d%d" % c), res("wf_bf")], [rtb])
                    if i % 2 == 0:
                        sc.op("dve", lambda e: e.tensor_copy(out=Ys[:, ysl, :], in_=tb), [rtb], [ry])
                    else:
                        sc.op("act", lambda e: e.activation(out=Ys[:, ysl, :], in_=tb, func=AF.Copy), [rtb], [ry])
                    return ysl

                def stepS(i, ysl):
                    ry = res("Ys%d" % ysl)
                    sb_, rsb = bS
                    kk = (i % 2) * 2

                    def fS(e):
                        ins = None
                        for j in range(2):
                            o_ = sb_[:, (kk + j) * 128:(kk + j + 1) * 128]
                            e.matmul(o_, lhsT=Ys[:, ysl, (j * 2) * 128:(j * 2 + 1) * 128], rhs=bd[:, 0:128],
                                     start=True, stop=False)
                            ins = e.matmul(o_, lhsT=Ys[:, ysl, (j * 2 + 1) * 128:(j * 2 + 2) * 128],
                                           rhs=bd[:, 128:256], start=False, stop=True)
                        return ins
                    sc.op("pe", fS, [ry], [rsb])
                    if i % 2 == 1 or i == NP - 1:
                        k0 = (i // 2) * 4
                        nk = kk + 2
                        dsts = FNp[:, g, k0:k0 + nk, :].rearrange("p k m -> p (k m)")
                        srcs = sb_[:, 0:nk * 128]
                        if (i // 2) % 2 == 0:
                            sc.op("act", lambda e: e.activation(out=dsts, in_=srcs, func=AF.Copy), [rsb], [r_uf[g]])
                        else:
                            sc.op("dve", lambda e: e.tensor_copy(out=dsts, in_=srcs), [rsb], [r_uf[g]])

                pendT = stepT(0)
                for i in range(NP):
                    curT = pendT
                    if i + 1 < NP:
                        pendT = stepT(i + 1)
                    stepS(i, curT)

            for g0 in (0, 2):
                sc.zipper(sc.record(fft_chain, g0, 0), sc.record(fft_chain, g0 + 1, 1))

            full_barrier()
            xslot = {}

            def H1(t):
                par = t % 2
                xp = t % 3
                if t == 0:
                    xslot[0] = load_x(base)
                if t + 1 < NT:
                    xslot[t + 1] = load_x(base + (t + 1) * 128)
                make_xT(xslot[t], xp)
                qkb, rqk = bank(0)
                proj_feat(xp, OGF, 32, misc[0:32, 0:128], res("m_g"))
                proj_tok(xp, OQ, 512, qkb, rqk)
                sc.op("act", lambda e: e.activation(out=gT2[0:32, par, :], in_=misc[0:32, 0:128], func=AF.Copy),
                      [res("m_g")], [res("gT%d" % par)])
                gates(par, 512, 0, 1)
                cb, rcb = bank(1)

                def fcum(e):
                    e.matmul(cb[:, 0:256], lhsT=tri[:, 0:128], rhs=e1[:, 0:256], start=True, stop=True)
                    return e.matmul(cb[:, 256:512], lhsT=tri[:, 128:256], rhs=e1[:, 256:512],
                                    start=True, stop=True)
                sc.op("pe", fcum, [res("e1")], [rcb])

                def ftot2(e):
                    for c in range(4):
                        ins = e.matmul(misc[:, 128 + c:129 + c], lhsT=e1[:, c * 128:(c + 1) * 128],
                                       rhs=ones[:, 0:1], start=True, stop=True)
                    return ins
                sc.op("pe", ftot2, [res("e1")], [res("m_tot")])
                sc.op("act", lambda e: e.activation(out=Eq, in_=cb, func=AF.Exp, scale=-1.0 / 16.0,
                                                    bias=math.log(0.125)), [rcb], [res("Eq")])
                sc.op("act", lambda e: e.activation(out=Ek, in_=cb, func=AF.Exp, scale=1.0 / 16.0),
                      [rcb], [res("Ek")])
                sc.op("act", lambda e: e.activation(out=sdec2[:, par, 0:4], in_=misc[:, 128:132], func=AF.Exp,
                                                    scale=-1.0 / 16.0), [res("m_tot")], [res("sdec%d" % par)])
                for w2 in range(2):
                    E_ = Eq if w2 == 0 else Ek
                    sc.op("dve", lambda e, w2=w2, E_=E_: e.tensor_tensor(
                        out=QK2[:, par, 2 * w2:2 * w2 + 2, :],
                        in0=qkb[:, w2 * 256:(w2 + 1) * 256].unsqueeze(1).to_broadcast([128, 2, 256]),
                        in1=E_.rearrange("p (a b) -> p a b", a=2), op=ALU.mult),
                        [rqk, res("Eq"), res("Ek")],
                        [res("QK%d_%d" % (2 * w2, par)), res("QK%d_%d" % (2 * w2 + 1, par))], cost=0.7)

                vb, rvb = bank(1)
                proj_tok(xp, OV, 512, vb, rvb)
                sc.op("act", lambda e: e.activation(out=vbf2[:, par, :], in_=vb, func=AF.Copy),
                      [rvb], [res("v_bf%d" % par)])

                def ftr(e):
                    ins = None
                    for w_ in range(4):
                        for c in range(2):
                            ins = e.transpose(out=pb1[:, w_ * 2 + c, :],
                                              in_=QK2[:, par, w_, c * 128:(c + 1) * 128], identity=ident)
                    return ins
                sc.op("pe", ftr, [res("QK%d_%d" % (w_, par)) for w_ in range(4)], [res("pb1")])
                sc.op("dve", lambda e: e.tensor_copy(out=KDT2[:, par], in_=pb1[:, 4:8, :]),
                      [res("pb1")], [res("KDT%d" % par)])
                sc.op("dve", lambda e: e.tensor_copy(out=QEz2[0:64, par, 0, :, :], in_=pb1[0:64, 0:4, :]),
                      [res("pb1"), res("QEz_init")], [res("QEz0_%d" % par)])
                sc.op("dve", lambda e: e.tensor_copy(out=QEz2[64:128, par, 1, :, :], in_=pb1[64:128, 0:4, :]),
                      [res("pb1"), res("QEz_init")], [res("QEz1_%d" % par)])

            def H2a(t):
                par = t % 2
                xp = t % 3
                mixT = mixT2[:, par]
                ef = ef2[:, par, :]
                KDT = KDT2[:, par]
                QEz = QEz2[:, par]
                v_bf = vbf2[:, par, :]
                rq = [res("KDT%d" % par), res("QEz0_%d" % par), res("QEz1_%d" % par)]
                rvb_ = res("v_bf%d" % par)
                zgb, rzg = bank(2)
                proj_tok(xp, OZG, 512, zgb, rzg)
                sc.op("act", lambda e: e.activation(out=eg, in_=zgb, func=AF.Exp, scale=-1.0), [rzg], [res("eg")])
                sc.op("act", lambda e: e.activation(out=eg, in_=eg, func=AF.Ln, bias=1.0), [res("eg")], [res("eg")])
                sc.op("act", lambda e: e.activation(out=eg, in_=eg, func=AF.Exp, scale=-1.0), [res("eg")], [res("eg")])
                sc.op("dve", lambda e: e.tensor_tensor(out=eg, in0=zgb, in1=eg, op=ALU.mult),
                      [rzg, res("eg")], [res("eg")])
                sc.op("pool", lambda e: e.tensor_tensor(out=eg, in0=eg, in1=gain, op=ALU.mult),
                      [res("eg")], [res("eg")])
                for d_ in range(2):
                    b_, rb_ = bank(2 + d_)

                    def fsc(e, b_=b_, d_=d_):
                        ins = None
                        for h in range(4):
                            c = h // 2
                            ins = e.matmul(b_[:, h * 128:(h + 1) * 128], lhsT=KDT[:, d_ * 2 + c, :],
                                           rhs=QEz[:, h % 2, d_ * 2 + c, :], start=True, stop=True)
                        return ins
                    sc.op("pe", fsc, rq, [rb_])
                    msk = tri[:, d_ * 128:(d_ + 1) * 128]
                    sc.op("dve", lambda e, b_=b_, d_=d_, msk=msk: e.tensor_tensor(
                        out=scm[:, d_, :].rearrange("p (h i) -> p h i", h=4),
                        in0=b_.rearrange("p (h i) -> p h i", h=4),
                        in1=msk.unsqueeze(1).to_broadcast([128, 4, 128]), op=ALU.mult),
                        [rb_], [res("scm%d" % d_)], cost=0.7)
                ob, rob = bank(2)

                def fo(e):
                    ins = None
                    for h in range(4):
                        c = h // 2
                        o_ = ob[:, h * 128:(h + 1) * 128]
                        e.matmul(o_, lhsT=scm[:, 0, h * 128:(h + 1) * 128], rhs=v_bf[:, h * 128:(h + 1) * 128],
                                 start=True, stop=False)
                        e.matmul(o_, lhsT=scm[:, 1, h * 128:(h + 1) * 128], rhs=v_bf[:, h * 128:(h + 1) * 128],
                                 start=False, stop=False)
                        e.matmul(o_, lhsT=QEz[:, h % 2, 0 + c, :], rhs=Sf_bf[:, c * 128:(c + 1) * 128],
                                 start=False, stop=False)
                        ins = e.matmul(o_, lhsT=QEz[:, h % 2, 2 + c, :],
                                       rhs=Sbs[:, t, c * 128:(c + 1) * 128], start=False, stop=True)
                    return ins
                sc.op("pe", fo, [res("scm0"), res("scm1"), rvb_, rq[1], rq[2], res("Sf_bf"), res("Sbs")], [rob], cost=1.1)
                state_update(QK2[:, par, 2, :], res("QK2_%d" % par), v_bf, rvb_, Sf, res("Sf"),
                             sdec2[:, par, 0:2], res("sdec%d" % par), 3)
                sc.op("pool", lambda e: e.tensor_copy(out=Sf_bf, in_=Sf), [res("Sf")], [res("Sf_bf")])
                for h in range(4):
                    sc.op("act", lambda e, h=h: e.activation(out=junk, in_=ob[:, h * 128:(h + 1) * 128],
                                                             func=AF.Square, accum_out=ss[:, h:h + 1]),
                          [rob], [res("ss%d" % h), res("junk")])
                sc.op("act", lambda e: e.activation(out=rstd, in_=ss, func=AF.Ln, scale=1.0 / 128.0, bias=RMS_EPS),
                      [res("ss%d" % h) for h in range(4)], [res("rstd")])
                sc.op("act", lambda e: e.activation(out=rstd, in_=rstd, func=AF.Exp, scale=-0.5),
                      [res("rstd")], [res("rstd")])
                for h in range(4):
                    sc.op("dve", lambda e, h=h: e.scalar_tensor_tensor(
                        out=yg[:, h * 128:(h + 1) * 128], in0=ob[:, h * 128:(h + 1) * 128],
                        scalar=rstd[:, h:h + 1], in1=eg[:, h * 128:(h + 1) * 128], op0=ALU.mult, op1=ALU.mult),
                        [rob, res("rstd"), res("eg")], [res("yg")])

                def ftry(e):
                    for h in range(4):
                        ins = e.transpose(out=ygT_ps[:, h, :], in_=yg[:, h * 128:(h + 1) * 128], identity=ident)
                    return ins
                sc.op("pe", ftry, [res("yg")], [res("m_ygT")])
                sc.op("act", lambda e: e.activation(out=mixT[:, 0:4, :], in_=ygT_ps, func=AF.Copy),
                      [res("m_ygT")], [res("mixT_a%d" % par)])

            def H2b(t):
                par = t % 2
                xsl = xslot[t]
                mixT = mixT2[:, par]
                ef = ef2[:, par, :]
                xp = t % 3
                zfb, rzf = bank(4)
                for g in range(4):
                    proj_feat(xp, OZF + g * 128, 128, zfb[:, g * 128:(g + 1) * 128], rzf)
                sc.op("act", lambda e: e.activation(out=ef, in_=zfb, func=AF.Exp, scale=-1.0), [rzf], [res("ef%d" % par)])
                sc.op("act", lambda e: e.activation(out=ef, in_=ef, func=AF.Ln, bias=1.0), [res("ef%d" % par)], [res("ef%d" % par)])
                sc.op("act", lambda e: e.activation(out=ef, in_=ef, func=AF.Exp, scale=-1.0), [res("ef%d" % par)], [res("ef%d" % par)])
                sc.op("dve", lambda e: e.tensor_tensor(out=ef, in0=zfb, in1=ef, op=ALU.mult),
                      [rzf, res("ef%d" % par)], [res("ef%d" % par)])
                sc.op("pool", lambda e: e.tensor_tensor(
                    out=mixT[:, 4:8, :].rearrange("p g (a k) -> p g a k", k=N2),
                    in0=ef.rearrange("p (g a k) -> p g a k", g=4, k=N2),
                    in1=FNp[:, :, :, t * Q:(t + 1) * Q].rearrange("p g k a -> p g a k"), op=ALU.mult),
                    [res("ef%d" % par)] + r_uf, [res("mixT_b%d" % par)], cost=1.5)
                for hf in range(2):
                    b_, rb_ = bank(4)

                    def fy(e, b_=b_, hf=hf):
                        for ec in range(8):
                            ins = e.matmul(b_, lhsT=mixT[:, ec, :], rhs=wout_bf[:, ec, hf * 512:(hf + 1) * 512],
                                           start=(ec == 0), stop=(ec == 7))
                        return ins
                    sc.op("pe", fy, [res("mixT_a%d" % par), res("mixT_b%d" % par)], [rb_], cost=1.8)
                    sc.op("dve", lambda e, b_=b_, hf=hf: e.scalar_tensor_tensor(
                        out=rt[:, hf * 512:(hf + 1) * 512], in0=xs[xsl][:, hf * 512:(hf + 1) * 512], scalar=ALPHA,
                        in1=b_, op0=ALU.mult, op1=ALU.add), [r_x[xsl], rb_], [res("rt%d" % hf)], cost=0.7)
                    sc.op("dve", lambda e, hf=hf: e.bn_stats(out=bst[:, hf, :], in_=rt[:, hf * 512:(hf + 1) * 512]),
                          [res("rt%d" % hf)], [res("bst%d" % hf)], cost=0.7)
                sc.op("dve", lambda e: e.bn_aggr(out=mv, in_=bst), [res("bst0"), res("bst1")], [res("mv")])
                sc.op("act", lambda e: e.activation(out=lnr[:, 0:1], in_=mv[:, 1:2], func=AF.Ln, bias=LN_EPS),
                      [res("mv")], [res("lnr0")])
                sc.op("act", lambda e: e.activation(out=lnr[:, 0:1], in_=lnr[:, 0:1], func=AF.Exp, scale=-0.5),
                      [res("lnr0")], [res("lnr0")])
                sc.op("dve", lambda e: e.scalar_tensor_tensor(out=lnr[:, 1:2], in0=mv[:, 0:1], scalar=-1.0,
                                                              in1=lnr[:, 0:1], op0=ALU.mult, op1=ALU.mult),
                      [res("mv"), res("lnr0")], [res("lnr1")])
                osl = ocount[0] % 2
                ocount[0] += 1
                sc.op("act", lambda e: e.activation(out=ot[osl], in_=rt, func=AF.Identity,
                                                    scale=lnr[:, 0:1], bias=lnr[:, 1:2]),
                      [res("rt0"), res("rt1"), res("lnr0"), res("lnr1")], [r_ot[osl]], cost=1.25)
                sc.op("pool", lambda e: e.tensor_tensor(out=ot[osl], in0=ot[osl], in1=lng, op=ALU.mult),
                      [r_ot[osl]], [r_ot[osl]], cost=2.35)
                sc.op("pool", lambda e: e.tensor_tensor(out=ot[osl], in0=ot[osl], in1=lnb, op=ALU.add),
                      [r_ot[osl]], [r_ot[osl]], cost=2.35)
                dma("sp", out_d[base + t * 128:base + (t + 1) * 128, :], ot[osl], [r_ot[osl]], [], "d_o%d" % osl)

            H1(0)
            if NT > 1:
                sc.zipper(sc.record(H1, 1), sc.record(H2a, 0))
            else:
                H2a(0)
            for t in range(NT):
                lists = []
                pr = []
                if t + 2 < NT:
                    lists.append(sc.record(H1, t + 2))
                    pr.append(PRIO2[0])
                if t + 1 < NT:
                    lists.append(sc.record(H2a, t + 1))
                    pr.append(PRIO2[1])
                lists.append(sc.record(H2b, t))
                pr.append(PRIO2[2])
                sc.zipper(*lists, prio=pr, sched=SCHED2)

        sc.wait_all("sp", [("d_o0", sc.cnt.get("d_o0", 0)), ("d_o1", sc.cnt.get("d_o1", 0))])

        for key in sc.cnt:
            sc.sems[key] = es.enter_context(nc.semaphore("m_" + key))
        with nc.Block() as block:
            @block.tensor
            def _(e):
                sc.emit("pe", e)

            @block.scalar
            def _(e):
                sc.emit("act", e)

            @block.vector
            def _(e):
                sc.emit("dve", e)

            @block.gpsimd
            def _(e):
                sc.emit("pool", e)

            @block.sync
            def _(e):
                sc.emit("sp", e)
    return nc


def _shared_inputs(w_in, w_gate_up_fwd, b_gate_fwd, w_gate_up_bwd, b_gate_bwd, gla_norm_g,
                   w_fnet, w_out, ln_g, ln_b, S):
    f = np.float32
    wup = np.zeros((33, 512), f)
    wup[0:16, 0:256] = np.asarray(w_gate_up_fwd, f)[0]
    wup[16:32, 256:512] = np.asarray(w_gate_up_bwd, f)[0]
    wup[32, 0:256] = np.asarray(b_gate_fwd, f)[0]
    wup[32, 256:512] = np.asarray(b_gate_bwd, f)[0]
    wf = np.ascontiguousarray(np.transpose(np.asarray(w_fnet, f)[0], (1, 0, 2)).reshape(128, 512))
    d = dict(
        w_in=np.ascontiguousarray(np.asarray(w_in, f)[0]),
        w_out=np.ascontiguousarray(np.asarray(w_out, f)[0]),
        wup=wup,
        gain=np.ascontiguousarray(np.asarray(gla_norm_g, f)[0].reshape(1, 512)),
        w_fnet=wf,
        ln_g=np.ascontiguousarray(np.asarray(ln_g, f)[0].reshape(1, D)),
        ln_b=np.ascontiguousarray(np.asarray(ln_b, f)[0].reshape(1, D)),
    )
    d.update(_tables(S))
    return d


def kernel(x, w_in, w_gate_up_fwd, b_gate_fwd, w_gate_up_bwd, b_gate_bwd, gla_norm_g,
           w_fnet, w_out, ln_g, ln_b):
    x = np.asarray(x, np.float32)
    B, S, _ = x.shape
    nseq = B // NCORES
    shared = _shared_inputs(w_in, w_gate_up_fwd, b_gate_fwd, w_gate_up_bwd, b_gate_bwd, gla_norm_g,
                            w_fnet, w_out, ln_g, ln_b, S)
    nc = build(S, nseq)
    in_maps = []
    for c in range(NCORES):
        m = dict(shared)
        m["x"] = np.ascontiguousarray(x[c * nseq:(c + 1) * nseq].reshape(nseq * S, D))
        in_maps.append(m)
    res = run_bass_kernel_spmd(nc, in_maps, core_ids=list(range(NCORES)))
    outs = [np.asarray(r["out"], np.float32).reshape(nseq, S, D) for r in res.results]
    return np.concatenate(outs, axis=0)
```
